# Optimizing a Trainium2 kernel written in Bass

```python
import math
import jax, jax.numpy as jnp
from jax import lax
import numpy as np

D_MODEL = 1024
BATCH = 8
SEQ = 2048
DEPTH = 1
DEC_BATCH = 128
DEC_SEQ = 4
PAST_LEN = 16384
PAGE_SIZE = 128

N_META = 16
S5_WIDTH = D_MODEL // 2
S5_GROUP = 16
S5_GROUPS = S5_WIDTH // S5_GROUP
S5_STATE = 64
GDN_HEADS = 4
GDN_DK = 128
GDN_DV = 128
GDN_KW = GDN_HEADS * GDN_DK
GDN_VW = GDN_HEADS * GDN_DV
QKV_W = 2 * GDN_KW + GDN_VW
CONV_W = 4
CHUNK = 64
D_FF = ((8 * D_MODEL + 3 * 256 - 1) // (3 * 256)) * 256
IN_W = S5_WIDTH + QKV_W + GDN_VW + 2 * GDN_HEADS + 2 * D_MODEL
EPS = 1e-6

kernel_name = "hybrid_s5_gated_deltanet_step"

F32 = jnp.float32


def rms_norm(x, g):
    xf = x.astype(F32)
    y = xf * lax.rsqrt(jnp.mean(xf * xf, axis=-1, keepdims=True) + EPS)
    return (y * g.astype(F32)).astype(x.dtype)


def l2_norm(x):
    return x * lax.rsqrt(jnp.sum(x * x, axis=-1, keepdims=True) + EPS)


def s5_mix(u, h0_re, h0_im, A_re, A_im, B_re, B_im, C_re, C_im, log_dt, D):
    b, l, _ = u.shape
    uf = u.astype(F32)
    ug = uf.reshape(b, l, S5_GROUPS, S5_GROUP).astype(jnp.complex64)
    lam = lax.complex(A_re.astype(F32), A_im.astype(F32))
    dt = jnp.exp(log_dt.astype(F32))[:, None]
    a_bar = jnp.exp(lam * dt)
    B = lax.complex(B_re.astype(F32), B_im.astype(F32))
    B_bar = ((a_bar - 1.0) / lam)[..., None] * B
    bu = jnp.einsum('gpc,blgc->blgp', B_bar, ug)
    h0 = lax.complex(h0_re.astype(F32), h0_im.astype(F32))
    bu = bu.at[:, 0].add(a_bar[None] * h0)
    a_seq = jnp.broadcast_to(a_bar, bu.shape)

    def combine(e1, e2):
        a1, b1 = e1
        a2, b2 = e2
        return a1 * a2, a2 * b1 + b2

    _, h = lax.associative_scan(combine, (a_seq, bu), axis=1)
    C = lax.complex(C_re.astype(F32), C_im.astype(F32))
    y = jnp.real(jnp.einsum('gcp,blgp->blgc', C, h)).reshape(b, l, S5_WIDTH) + D.astype(F32) * uf
    h_last = h[:, -1]
    return y.astype(u.dtype), jnp.real(h_last), jnp.imag(h_last)


def short_conv(x, buf, w):
    l = x.shape[1]
    xp = jnp.concatenate([buf.astype(x.dtype), x], axis=1)
    y = w[0] * xp[:, 0:l]
    for j in range(1, CONV_W):
        y = y + w[j] * xp[:, j:j + l]
    return jax.nn.silu(y), xp[:, xp.shape[1] - (CONV_W - 1):]


def gdn_segment(q, k, v, g, beta, S0):
    b, l, h, dk = q.shape
    dv = v.shape[-1]
    c = min(CHUNK, l)
    n = -(-l // c)
    pad = n * c - l
    if pad:
        padf = lambda t: jnp.pad(t, [(0, 0), (0, pad)] + [(0, 0)] * (t.ndim - 2))
        q, k, v, g, beta = padf(q), padf(k), padf(v), padf(g), padf(beta)

    def to_chunks(t):
        return jnp.moveaxis(t.reshape((b, n, c) + t.shape[2:]), 3, 1)

    q, k, v, g, beta = to_chunks(q), to_chunks(k), to_chunks(v), to_chunks(g), to_chunks(beta)
    gc = jnp.cumsum(g, axis=-1)
    idx = jnp.arange(c)
    causal = idx[:, None] >= idx[None, :]
    strict = idx[:, None] > idx[None, :]
    decay = jnp.exp(jnp.where(causal, gc[..., :, None] - gc[..., None, :], -jnp.inf))
    kb = k * beta[..., None]
    m = jnp.where(strict, jnp.einsum('bhnid,bhnjd->bhnij', kb, k) * decay, 0.0)
    eye = jnp.eye(c, dtype=F32)
    t = lax.linalg.triangular_solve(eye + m, jnp.broadcast_to(eye, m.shape), left_side=True, lower=True)
    u = t @ (v * beta[..., None])
    w = t @ (kb * jnp.exp(gc)[..., None])
    qk = jnp.einsum('bhnid,bhnjd->bhnij', q, k) * decay
    q_dec = q * jnp.exp(gc)[..., None]
    g_last = gc[..., -1]
    k_dec = k * jnp.exp(g_last[..., None] - gc)[..., None]

    def step(S, xs):
        u_i, w_i, qk_i, qd_i, kd_i, gl_i = xs
        v_new = u_i - w_i @ S
        o = qd_i @ S + qk_i @ v_new
        S = S * jnp.exp(gl_i)[..., None, None] + jnp.einsum('bhcd,bhce->bhde', kd_i, v_new)
        return S, o

    xs = (jnp.moveaxis(u, 2, 0), jnp.moveaxis(w, 2, 0), jnp.moveaxis(qk, 2, 0),
          jnp.moveaxis(q_dec, 2, 0), jnp.moveaxis(k_dec, 2, 0), jnp.moveaxis(g_last, 2, 0))
    S, o = lax.scan(step, S0, xs)
    o = jnp.transpose(o, (1, 0, 3, 2, 4)).reshape(b, n * c, h, dv)[:, :l]
    return o, S


def layer(x, h_re, h_im, s_gdn, conv_buf, splits, lp):
    b, l, _ = x.shape
    hn = rms_norm(x, lp['norm_mix_pre'])
    proj = hn @ lp['w_in']
    o1 = S5_WIDTH
    o2 = o1 + QKV_W
    o3 = o2 + GDN_VW
    o4 = o3 + GDN_HEADS
    o5 = o4 + GDN_HEADS
    u, qkv, z, a, bt, gates = jnp.split(proj, [o1, o2, o3, o4, o5], axis=-1)
    y_s5, h_re_n, h_im_n = s5_mix(u, h_re, h_im, lp['s5_A_re'], lp['s5_A_im'], lp['s5_B_re'], lp['s5_B_im'],
                                  lp['s5_C_re'], lp['s5_C_im'], lp['s5_log_dt'], lp['s5_D'])
    gs = jax.nn.gelu(y_s5)
    br_a = (gs @ lp['w_s5_glu_a']) * jax.nn.sigmoid(gs @ lp['w_s5_glu_b'])
    qkv_c, conv_new = short_conv(qkv, conv_buf, lp['gdn_conv_w'])
    qkv_c = qkv_c.astype(F32)
    q = l2_norm(qkv_c[..., :GDN_KW].reshape(b, l, GDN_HEADS, GDN_DK)) * (GDN_DK ** -0.5)
    k = l2_norm(qkv_c[..., GDN_KW:2 * GDN_KW].reshape(b, l, GDN_HEADS, GDN_DK))
    v = qkv_c[..., 2 * GDN_KW:].reshape(b, l, GDN_HEADS, GDN_DV)
    g = -jnp.exp(lp['gdn_A_log'].astype(F32)) * jax.nn.softplus(a.astype(F32) + lp['gdn_dt_bias'].astype(F32))
    beta = jax.nn.sigmoid(bt.astype(F32))
    S = s_gdn.astype(F32)
    bounds = (0,) + tuple(splits) + (l,)
    outs = []
    for s0, s1 in zip(bounds[:-1], bounds[1:]):
        o_seg, S = gdn_segment(q[:, s0:s1], k[:, s0:s1], v[:, s0:s1], g[:, s0:s1], beta[:, s0:s1], S)
        outs.append(o_seg)
    o = jnp.concatenate(outs, axis=1)
    zf = z.astype(F32).reshape(b, l, GDN_HEADS, GDN_DV)
    o = rms_norm(o, lp['gdn_norm']) * jax.nn.silu(zf)
    br_b = o.reshape(b, l, GDN_VW).astype(x.dtype) @ lp['w_gdn_out']
    ga, gb = jnp.split(gates, 2, axis=-1)
    mix = (jax.nn.sigmoid(ga) * br_a + jax.nn.sigmoid(gb) * br_b) @ lp['w_out']
    x = x + rms_norm(mix, lp['norm_mix_post'])
    f = rms_norm(x, lp['norm_ffn_pre'])
    f = (jax.nn.silu(f @ lp['w_ffn_gate']) * (f @ lp['w_ffn_up'])) @ lp['w_ffn_down']
    x = x + rms_norm(f, lp['norm_ffn_post'])
    return x, h_re_n, h_im_n, S, conv_new


def setup_inputs(seed: int = 0) -> dict:
    key = jax.random.key(seed)
    ks = jax.random.split(key, 40)
    nrm = lambda i, shape, s=1.0: jax.random.normal(ks[i], shape, F32) * s
    gain = lambda i, shape: 1.0 + 0.02 * jax.random.normal(ks[i], shape, F32)
    n_idx = jnp.arange(S5_STATE, dtype=F32)
    dt_gdn = jnp.exp(jax.random.uniform(ks[30], (DEPTH, GDN_HEADS), F32, math.log(0.001), math.log(0.1)))
    return {
        'x_prompt': nrm(0, (BATCH, SEQ, D_MODEL)),
        'x_sample': nrm(1, (DEC_BATCH, DEC_SEQ, D_MODEL)),
        'state_s5_re': nrm(2, (DEPTH, DEC_BATCH, S5_GROUPS, S5_STATE), 0.1),
        'state_s5_im': nrm(3, (DEPTH, DEC_BATCH, S5_GROUPS, S5_STATE), 0.1),
        'state_gdn': nrm(4, (DEPTH, DEC_BATCH, GDN_HEADS, GDN_DK, GDN_DV), 0.1),
        'state_conv': nrm(5, (DEPTH, DEC_BATCH, CONV_W - 1, QKV_W)),
        'meta_tokens': nrm(6, (N_META, D_MODEL)),
        'norm_mix_pre': gain(7, (DEPTH, D_MODEL)),
        'norm_mix_post': gain(8, (DEPTH, D_MODEL)),
        'norm_ffn_pre': gain(9, (DEPTH, D_MODEL)),
        'norm_ffn_post': gain(10, (DEPTH, D_MODEL)),
        'w_in': nrm(11, (DEPTH, D_MODEL, IN_W), D_MODEL ** -0.5),
        's5_A_re': -0.5 + nrm(12, (DEPTH, S5_GROUPS, S5_STATE), 0.01),
        's5_A_im': math.pi * n_idx + nrm(13, (DEPTH, S5_GROUPS, S5_STATE), 0.01),
        's5_B_re': nrm(14, (DEPTH, S5_GROUPS, S5_STATE, S5_GROUP), (2 * S5_GROUP) ** -0.5),
        's5_B_im': nrm(15, (DEPTH, S5_GROUPS, S5_STATE, S5_GROUP), (2 * S5_GROUP) ** -0.5),
        's5_C_re': nrm(16, (DEPTH, S5_GROUPS, S5_GROUP, S5_STATE), (2 * S5_STATE) ** -0.5),
        's5_C_im': nrm(17, (DEPTH, S5_GROUPS, S5_GROUP, S5_STATE), (2 * S5_STATE) ** -0.5),
        's5_log_dt': jax.random.uniform(ks[18], (DEPTH, S5_GROUPS), F32, math.log(0.001), math.log(0.1)),
        's5_D': nrm(19, (DEPTH, S5_WIDTH)),
        'w_s5_glu_a': nrm(20, (DEPTH, S5_WIDTH, D_MODEL), S5_WIDTH ** -0.5),
        'w_s5_glu_b': nrm(21, (DEPTH, S5_WIDTH, D_MODEL), S5_WIDTH ** -0.5),
        'gdn_conv_w': nrm(22, (DEPTH, CONV_W, QKV_W), CONV_W ** -0.5),
        'gdn_A_log': jnp.log(jax.random.uniform(ks[23], (DEPTH, GDN_HEADS), F32, 1.0, 16.0)),
        'gdn_dt_bias': dt_gdn + jnp.log(-jnp.expm1(-dt_gdn)),
        'gdn_norm': gain(24, (DEPTH, GDN_DV)),
        'w_gdn_out': nrm(25, (DEPTH, GDN_VW, D_MODEL), GDN_VW ** -0.5),
        'w_out': nrm(26, (DEPTH, D_MODEL, D_MODEL), D_MODEL ** -0.5),
        'w_ffn_gate': nrm(27, (DEPTH, D_MODEL, D_FF), D_MODEL ** -0.5),
        'w_ffn_up': nrm(28, (DEPTH, D_MODEL, D_FF), D_MODEL ** -0.5),
        'w_ffn_down': nrm(29, (DEPTH, D_FF, D_MODEL), D_FF ** -0.5),
    }


def reference(x_prompt, x_sample, state_s5_re, state_s5_im, state_gdn, state_conv, meta_tokens,
              norm_mix_pre, norm_mix_post, norm_ffn_pre, norm_ffn_post, w_in,
              s5_A_re, s5_A_im, s5_B_re, s5_B_im, s5_C_re, s5_C_im, s5_log_dt, s5_D,
              w_s5_glu_a, w_s5_glu_b, gdn_conv_w, gdn_A_log, gdn_dt_bias, gdn_norm, w_gdn_out,
              w_out, w_ffn_gate, w_ffn_up, w_ffn_down):
    xp = jnp.concatenate([jnp.broadcast_to(meta_tokens.astype(x_prompt.dtype)[None], (BATCH, N_META, D_MODEL)),
                          x_prompt], axis=1)
    xs = x_sample
    p_re, p_im, p_gdn, p_conv = [], [], [], []
    s_re, s_im, s_gdn, s_conv = [], [], [], []
    for i in range(DEPTH):
        lp = dict(norm_mix_pre=norm_mix_pre[i], norm_mix_post=norm_mix_post[i],
                  norm_ffn_pre=norm_ffn_pre[i], norm_ffn_post=norm_ffn_post[i], w_in=w_in[i],
                  s5_A_re=s5_A_re[i], s5_A_im=s5_A_im[i], s5_B_re=s5_B_re[i], s5_B_im=s5_B_im[i],
                  s5_C_re=s5_C_re[i], s5_C_im=s5_C_im[i], s5_log_dt=s5_log_dt[i], s5_D=s5_D[i],
                  w_s5_glu_a=w_s5_glu_a[i], w_s5_glu_b=w_s5_glu_b[i], gdn_conv_w=gdn_conv_w[i],
                  gdn_A_log=gdn_A_log[i], gdn_dt_bias=gdn_dt_bias[i], gdn_norm=gdn_norm[i],
                  w_gdn_out=w_gdn_out[i], w_out=w_out[i], w_ffn_gate=w_ffn_gate[i],
                  w_ffn_up=w_ffn_up[i], w_ffn_down=w_ffn_down[i])
        zh = jnp.zeros((BATCH, S5_GROUPS, S5_STATE), F32)
        zs = jnp.zeros((BATCH, GDN_HEADS, GDN_DK, GDN_DV), F32)
        zc = jnp.zeros((BATCH, CONV_W - 1, QKV_W), xp.dtype)
        xp, hr, hi, sg, cb = layer(xp, zh, zh, zs, zc, (N_META,), lp)
        p_re.append(hr.astype(xp.dtype)); p_im.append(hi.astype(xp.dtype))
        p_gdn.append(sg.astype(xp.dtype)); p_conv.append(cb.astype(xp.dtype))
        xs, hr, hi, sg, cb = layer(xs, state_s5_re[i], state_s5_im[i], state_gdn[i], state_conv[i], (), lp)
        s_re.append(hr.astype(state_s5_re.dtype)); s_im.append(hi.astype(state_s5_im.dtype))
        s_gdn.append(sg.astype(state_gdn.dtype)); s_conv.append(cb.astype(state_conv.dtype))
    y_prompt = xp[:, N_META:]
    y_sample = xs
    return (y_prompt, y_sample, jnp.stack(p_re), jnp.stack(p_im), jnp.stack(p_gdn), jnp.stack(p_conv),
            jnp.stack(s_re), jnp.stack(s_im), jnp.stack(s_gdn), jnp.stack(s_conv))
```

```python
import contextlib
import math
import numpy as np
import concourse.bass as bass
import concourse.mybir as mybir
from concourse.bass_utils import run_bass_kernel_spmd

F32 = mybir.dt.float32
BF16 = mybir.dt.bfloat16
AF = mybir.ActivationFunctionType
ALU = mybir.AluOpType
AX = mybir.AxisListType

D = 1024
SEQ = 2048
NMETA = 16
NSEQ_S = 16
LS = 4
S5W = 512
NG = 32
NP = 64
QKVW = 1536
NH = 4
DFF = 2816
INW = 4616
O_QKV = 512
O_Z = 2048
O_AB = 2560
O_G = 2568
EPS = 1e-6
ARENA = 9728
LT = 65
TWO_PI = 2.0 * math.pi


def ap_box(ap):
    b = _box(ap)
    z = mybir.dt.size(ap.dtype)
    return (b[0], b[1], b[2], b[3] * z, b[4] * z)


def _box(ap):
    dims = ap.ap
    off = int(ap.offset)
    space = str(ap.space)
    if space in ("SB", "PSUM"):
        pstep, pcount = dims[0]
        if pstep == 0:
            p0, f0, p1 = 0, off, 128
        else:
            p0 = off // pstep
            f0 = off % pstep
            p1 = p0 + pcount
        lo = hi = f0
        for s, c in dims[1:]:
            if s >= 0:
                hi += (c - 1) * s
            else:
                lo += (c - 1) * s
        return (ap.name, p0, p1, lo, hi + 1)
    lo = hi = off
    for s, c in dims:
        if s >= 0:
            hi += (c - 1) * s
        else:
            lo += (c - 1) * s
    return (ap.name, 0, 1, lo, hi + 1)


import os as _os
PSUM_BANK_READS = _os.environ.get("K_PSUMBANK", "1") != "0"


class Op:
    __slots__ = ("eng", "fn", "reads", "writes", "dma", "deps", "signal", "tick", "sem", "idx")


class Sched:
    ENGS = ("pe", "act", "dve", "pool", "sp")

    def __init__(self, nc, es, n_dma_sems=64):
        self.nc = nc
        self.es = es
        self.ops = []
        self.recs = {}
        self.n_dma_sems = n_dma_sems

    def add(self, eng, fn, reads=(), writes=(), dma=False):
        op = Op()
        op.eng, op.fn, op.dma = eng, fn, dma
        def rbox(a):
            b = ap_box(a)
            if PSUM_BANK_READS and str(a.space) == "PSUM":
                return (b[0], 0, 128, 0, 1 << 30)
            return b
        op.reads = [rbox(a) for a in reads]
        op.writes = [ap_box(a) for a in writes]
        op.deps, op.signal, op.tick, op.sem = set(), False, None, None
        op.idx = len(self.ops)
        self.ops.append(op)
        self._track(op)
        return op

    def _need(self, op, other_idx, kind):
        o = self.ops[other_idx]
        if o is op:
            return
        if (not op.dma) and (not o.dma) and o.eng == op.eng:
            if op.eng == "pe":
                return
        op.deps.add(other_idx)

    def _track(self, op):
        recs = self.recs
        for (name, p0, p1, f0, f1) in op.reads:
            lst = recs.setdefault(name, [])
            for r in lst:
                if r[1] and r[2] < p1 and p0 < r[3] and r[4] < f1 and f0 < r[5]:
                    self._need(op, r[0], "RAW")
            if not op.dma:
                for i in range(len(lst) - 1, -1, -1):
                    r = lst[i]
                    if (not r[1]) and r[2] == p0 and r[3] == p1 and r[4] == f0 and r[5] == f1:
                        o = self.ops[r[0]]
                        if (not o.dma) and o.eng == op.eng:
                            del lst[i]
            lst.append([op.idx, False, p0, p1, f0, f1])
        for (name, p0, p1, f0, f1) in op.writes:
            lst = recs.setdefault(name, [])
            keep = []
            for r in lst:
                if r[2] < p1 and p0 < r[3] and r[4] < f1 and f0 < r[5]:
                    self._need(op, r[0], "WAW" if r[1] else "WAR")
                    if r[2] >= p0 and r[3] <= p1 and r[4] >= f0 and r[5] <= f1:
                        continue
                keep.append(r)
            keep.append([op.idx, True, p0, p1, f0, f1])
            recs[name] = keep

    def emit(self):
        nc = self.nc
        engs = {"pe": nc.tensor, "act": nc.scalar, "dve": nc.vector, "pool": nc.gpsimd, "sp": nc.sync}
        esem = {e: self.es.enter_context(nc.semaphore("sem_" + e)) for e in self.ENGS}
        half = self.n_dma_sems // 2
        dsems = [self.es.enter_context(nc.semaphore("dsem%d" % i)) for i in range(self.n_dma_sems)]
        pool_of = {"sp": list(range(0, half)), "pool": list(range(half, self.n_dma_sems))}
        nd_q = {"sp": 0, "pool": 0}
        for op in self.ops:
            best = {}
            keep = set()
            for d in op.deps:
                o = self.ops[d]
                if o.dma:
                    keep.add(d)
                elif best.get(o.eng, -1) < d:
                    best[o.eng] = d
            keep.update(best.values())
            op.deps = keep
            for d in keep:
                self.ops[d].signal = True
        ticks = {e: 0 for e in self.ENGS}
        dcount = [0] * self.n_dma_sems
        waited = {}
        nd = 0
        n_wait = 0
        dma_final = {}
        for op in self.ops:
            eng = engs[op.eng]
            wl = {}
            for d in op.deps:
                o = self.ops[d]
                key = id(o.sem)
                if key not in wl or wl[key][1] < o.tick:
                    wl[key] = (o.sem, o.tick)
            if op.dma:
                ks = pool_of.get(op.eng, pool_of["sp"])
                k = ks[nd_q.get(op.eng, 0) % len(ks)]
                nd_q[op.eng] = nd_q.get(op.eng, 0) + 1
                nd += 1
                if dcount[k] > 0:
                    key = id(dsems[k])
                    v = dcount[k] * 16
                    if key not in wl or wl[key][1] < v:
                        wl[key] = (dsems[k], v)
                dcount[k] += 1
                op.sem = dsems[k]
                op.tick = dcount[k] * 16
                dma_final[k] = op.tick
            elif op.signal:
                ticks[op.eng] += 1
                op.sem = esem[op.eng]
                op.tick = ticks[op.eng]
            for key, (sem, val) in wl.items():
                wk = (op.eng, key)
                if waited.get(wk, 0) >= val:
                    continue
                waited[wk] = val
                eng.wait_ge(sem, val)
                n_wait += 1
            ins = op.fn(eng)
            if op.dma:
                ins.then_inc(op.sem, 16)
            elif op.signal:
                ins.then_inc(op.sem, 1)
        for k, v in dma_final.items():
            if waited.get(("sp", id(dsems[k])), 0) < v:
                nc.sync.wait_ge(dsems[k], v)
        self.stats = dict(n_ops=len(self.ops), n_wait=n_wait, ticks=dict(ticks), n_dma=nd)
        return self.stats


class Builder:
    def __init__(self, debug=None):
        self.debug = debug or []
        self.nc = bass.Bass("TRN2", target_bir_lowering=False)
        self.es = contextlib.ExitStack()
        self.s = Sched(self.nc, self.es)
        self.dbg_outs = {}
        self._n = 0

    def sb(self, name, shape, dt=F32):
        return self.es.enter_context(self.nc.sbuf_tensor(name, list(shape), dt))

    def ps(self, name, shape, dt=F32):
        return self.es.enter_context(self.nc.psum_tensor(name, list(shape), dt))

    def din(self, name, shape):
        return self.nc.dram_tensor(name, list(shape), F32, kind="ExternalInput").ap()

    def dout(self, name, shape):
        return self.nc.dram_tensor(name, list(shape), F32, kind="ExternalOutput").ap()

    def mm(self, out, lhsT, rhs, start=True, stop=True):
        self.s.add("pe", lambda e: e.matmul(out, lhsT, rhs, start=start, stop=stop),
                   reads=[lhsT, rhs], writes=[out])

    def tr(self, out, in_, ident):
        self.s.add("pe", lambda e: e.transpose(out, in_, ident), reads=[in_, ident], writes=[out])

    def act(self, out, in_, func, bias=None, scale=None, accum=None, eng="act"):
        reads = [in_]
        kw = {}
        if bias is not None:
            kw["bias"] = bias
            if not isinstance(bias, (int, float)):
                reads.append(bias)
        if scale is not None:
            kw["scale"] = scale
            if not isinstance(scale, (int, float)):
                reads.append(scale)
        writes = [out]
        if accum is not None:
            kw["accum_out"] = accum
            writes.append(accum)
        self.s.add("act", lambda e: e.activation(out, in_, func, **kw), reads=reads, writes=writes)

    def tt(self, out, a, b, op, eng="dve"):
        self.s.add(eng, lambda e: e.tensor_tensor(out, a, b, op), reads=[a, b], writes=[out])

    def ts(self, out, a, s1, op0, s2=None, op1=None, eng="dve", accum=None):
        reads = [a]
        if not isinstance(s1, (int, float)):
            reads.append(s1)
        if s2 is not None and not isinstance(s2, (int, float)):
            reads.append(s2)
        writes = [out]
        if accum is not None:
            writes.append(accum)
        if op1 is None:
            self.s.add(eng, lambda e: e.tensor_scalar(out, a, s1, None, op0), reads=reads, writes=writes)
        elif accum is None:
            self.s.add(eng, lambda e: e.tensor_scalar(out, a, s1, s2, op0, op1), reads=reads, writes=writes)
        else:
            self.s.add(eng, lambda e: e.tensor_scalar(out, a, s1, s2, op0, op1, accum_out=accum),
                       reads=reads, writes=writes)

    def stt(self, out, in0, scalar, in1, op0, op1):
        reads = [in0, in1]
        if not isinstance(scalar, (int, float)):
            reads.append(scalar)
        self.s.add("dve", lambda e: e.scalar_tensor_tensor(out, in0, scalar, in1, op0, op1),
                   reads=reads, writes=[out])

    def copy(self, out, in_, eng="dve"):
        if eng == "act":
            self.s.add("act", lambda e: e.copy(out, in_), reads=[in_], writes=[out])
        else:
            self.s.add(eng, lambda e: e.tensor_copy(out, in_), reads=[in_], writes=[out])

    def memset(self, ap, val, eng="dve"):
        self.s.add(eng, lambda e: e.memset(ap, val), writes=[ap])

    def recip(self, out, in_):
        self.s.add("dve", lambda e: e.reciprocal(out, in_), reads=[in_], writes=[out])

    def scan(self, out, d0, d1, init=0.0):
        self.s.add("dve", lambda e: e.tensor_tensor_scan(out, d0, d1, init, ALU.mult, ALU.add),
                   reads=[d0, d1], writes=[out])

    def dma(self, out, in_, eng="sp"):
        self.s.add(eng, lambda e: e.dma_start(out=out, in_=in_), reads=[in_], writes=[out], dma=True)

    def dbg(self, name, ap):
        if name not in self.debug or not getattr(self, "_dbg_on", True):
            return
        shp = list(ap.shape)
        d = self.dout("dbg_" + name, shp)
        self.dma(d, ap, eng="pool" if ap.dtype != F32 else "sp")
        self.dbg_outs[name] = "dbg_" + name


def mkap(ap, dims, offset_add=0):
    return bass.AP(ap.tensor, int(ap.offset) + offset_add, [list(d) for d in dims])


def bcast_last(ap, n):
    return mkap(ap, list(ap.ap) + [[0, n]])


def bcast_mid(ap, n):
    d = list(ap.ap)
    return mkap(ap, [d[0], [0, n]] + d[1:])


class Prog(Builder):
    def __init__(self, debug=None, st_list=None, phases=("s5", "gdn", "mix", "out", "ffn")):
        super().__init__(debug)
        self.st_list = st_list
        self.phases = phases
        self.io()
        self.consts()
        self.s5_setup()
        self.run_all()
        self.stats = self.s.emit()

    def io(self):
        d = self.din
        self.xp = d("xp", [NMETA + SEQ, D])
        self.xs = d("xs", [NSEQ_S * LS, D])
        self.s5re_in = d("s5re", [NSEQ_S * NG, NP])
        self.s5im_in = d("s5im", [NSEQ_S * NG, NP])
        self.sgdn_in = d("sgdn", [NSEQ_S * NH * 128, 128])
        self.sconv_in = d("sconv", [NSEQ_S * 3, QKVW])
        self.g_mix_pre = d("g_mix_pre", [1, D])
        self.g_mix_post = d("g_mix_post", [1, D])
        self.g_ffn_pre = d("g_ffn_pre", [1, D])
        self.g_ffn_post = d("g_ffn_post", [1, D])
        self.w_in = d("w_in", [D, INW])
        self.A_re = d("A_re", [NG, NP])
        self.A_im = d("A_im", [NG, NP])
        self.B_re = d("B_re", [NG, NP, 16])
        self.B_im = d("B_im", [NG, NP, 16])
        self.C_re = d("C_re", [NG * 16, NP])
        self.C_im = d("C_im", [NG * 16, NP])
        self.log_dt = d("log_dt", [1, NG])
        self.s5D = d("s5D", [1, S5W])
        self.w_ga = d("w_ga", [S5W, D])
        self.w_gb = d("w_gb", [S5W, D])
        self.conv_w = d("conv_w", [4, QKVW])
        self.A_log = d("A_log", [1, NH])
        self.dt_bias = d("dt_bias", [1, NH])
        self.gdn_norm = d("gdn_norm", [1, 128])
        self.w_gdn = d("w_gdn", [512, D])
        self.w_out = d("w_out", [D, D])
        self.w_fg = d("w_fg", [D, DFF])
        self.w_fu = d("w_fu", [D, DFF])
        self.w_fd = d("w_fd", [DFF, D])
        o = self.dout
        self.yp = o("yp", [SEQ, D])
        self.ys = o("ys", [NSEQ_S * LS, D])
        self.p_re = o("p_re", [NG, NP])
        self.p_im = o("p_im", [NG, NP])
        self.p_gdn = o("p_gdn", [NH * 128, 128])
        self.p_conv = o("p_conv", [3, QKVW])
        self.s_re = o("s_re", [NSEQ_S * NG, NP])
        self.s_im = o("s_im", [NSEQ_S * NG, NP])
        self.s_gdn = o("s_gdn", [NSEQ_S * NH * 128, 128])
        self.s_conv = o("s_conv", [NSEQ_S * 3, QKVW])

    def rows_to_cols(self, dram, R, C, dst):
        nat = self.nat8
        self.dma(nat[:R, :C], dram)
        nch = C // 128
        for c in range(nch):
            self.tr(self.ptf[:, c * 8:c * 8 + R], nat[:R, c * 128:(c + 1) * 128], self.ident_f[:R, :R])
        pv = mkap(self.ptf[:, 0:1], [[512, 128], [8, nch], [1, R]])
        self.copy(dst, pv)

    def consts(self):
        B = self
        sb, ps = self.sb, self.ps
        self.pm = [ps("pm%d" % i, [128, 512]) for i in range(4)]
        self.ptb = ps("ptb", [128, 1024], BF16)
        self.ptf = ps("ptf", [128, 512])
        self.psm = [ps("psm%d" % i, [128, 512]) for i in range(2)]
        self._pm_i = 0
        self._psm_i = 0
        self.ident_f = sb("ident_f", [128, 128])
        self.ident_bf = sb("ident_bf", [128, 128], BF16)
        self.ones_f = sb("ones_f", [128, 128])
        self.ones_bf = sb("ones_bf", [128, 128], BF16)
        self.UT = sb("UT", [64, 128])
        self.SU = sb("SU", [64, 64])
        self.SL = sb("SL", [64, 128])
        self.tail = sb("tail", [64, QKVW])
        self.nat8 = self.tail
        self.epsb = sb("epsb", [128, 1])
        B.memset(self.ones_f[:], 1.0, eng="pool")
        B.memset(self.ones_bf[:], 1.0, eng="pool")
        B.memset(self.epsb[:], EPS, eng="pool")
        self.onesb = self.ones_f[:, 0:1]
        sel = lambda out, in_, pat, op, cm: self.s.add(
            "pool", lambda e: e.affine_select(out, in_, pat, op, 0.0, base=0, channel_multiplier=cm),
            reads=[in_], writes=[out])
        sel(self.ident_f[:], self.ones_f[:], [[-1, 128]], ALU.is_equal, 1)
        B.memset(self.UT[:], 0.0, eng="pool")
        B.memset(self.SL[:], 0.0, eng="pool")
        sel(self.UT[:, 0:64], self.ones_f[0:64, 0:64], [[1, 64]], ALU.is_ge, -1)
        sel(self.SU[:], self.ones_f[0:64, 0:64], [[1, 64]], ALU.is_gt, -1)
        sel(self.SL[:, 0:64], self.ones_f[0:64, 0:64], [[-1, 64]], ALU.is_gt, 1)
        B.copy(self.ident_bf[:], self.ident_f[:])
        self.gpre = sb("gpre", [128, 8, 1])
        self.gfpre = sb("gfpre", [128, 8, 1])
        self.rows_to_cols(self.g_mix_pre, 1, D, self.gpre[:])
        self.rows_to_cols(self.g_ffn_pre, 1, D, self.gfpre[:])
        self.gbc = sb("gbc", [128, D])
        self.s5Dp = sb("s5Dp", [128, 4, 1])
        self.rows_to_cols(self.s5D, 1, S5W, self.s5Dp[:])
        self.cw = sb("cw", [128, 12, 4])
        self.rows_to_cols(self.conv_w, 4, QKVW, self.cw[:])
        self.gdnw = sb("gdnw", [128, 1, 1])
        self.rows_to_cols(self.gdn_norm, 1, 128, self.gdnw[:])
        self.negA = sb("negA", [128, NH])
        self.dtb = sb("dtb", [128, NH])
        B.dma(self.negA[:], mkap(self.A_log, [[0, 128], [1, NH]]))
        B.dma(self.dtb[:], mkap(self.dt_bias, [[0, 128], [1, NH]]))
        B.act(self.negA[:], self.negA[:], AF.Exp)
        B.ts(self.negA[:], self.negA[:], -1.0, ALU.mult)
        self.NWB = 3
        self.wb = [sb("wb%d" % i, [128, 4096], BF16) for i in range(self.NWB)]
        self._wb_i = 0

    def next_pm(self):
        p = self.pm[self._pm_i % 4]
        self._pm_i += 1
        return p

    def next_psm(self):
        p = self.psm[self._psm_i % 2]
        self._psm_i += 1
        return p

    def wload(self, dram2d, K, C):
        kc = K // 128
        buf = self.wb[self._wb_i % self.NWB]
        self._wb_i += 1
        v = mkap(buf[:, 0:1], [[4096, 128], [C, kc], [1, C]])
        src = dram2d.rearrange("(kc p) c -> p kc c", p=128)
        self.dma(v, src, eng="pool")
        return v

    def s5_setup(self):
        B = self
        sb = self.sb
        H = slice(0, 64)
        L = slice(64, 128)
        self.arena = sb("arena", [128, ARENA])
        ar = self.arena
        self._ar = 0

        def av(dims, dt=None):
            n = 1
            for d in dims:
                n *= d
            steps = []
            st = 1
            for d in reversed(dims):
                steps.append([st, d])
                st *= d
            v = mkap(ar[:, 0:1], [[ARENA, 128]] + steps[::-1], offset_add=self._ar)
            self._ar += n
            return v
        nat = sb("s5nat", [32, 128])
        lamR = sb("lamR", [128, NG])
        lamI = sb("lamI", [128, NG])
        for (src, dst) in ((self.A_re, lamR), (self.A_im, lamI)):
            B.dma(nat[:, 0:64], src)
            B.dma(nat[:, 64:128], src)
            B.tr(self.ptf[:, 0:32], nat[:, :], self.ident_f[:32, :32])
            B.copy(dst[:], self.ptf[:, 0:32])
        dt = sb("s5dt", [128, NG])
        B.dma(dt[:], mkap(self.log_dt, [[0, 128], [1, NG]]))
        ecst = sb("s5e", [128, NG])
        B.memset(ecst[:], math.e, eng="pool")
        B.tt(dt[:], ecst[:], dt[:], ALU.pow, eng="pool")
        th = sb("s5th", [128, NG])
        rho = sb("s5rho", [128, NG])
        B.tt(th[:], lamI[:], dt[:], ALU.mult)
        B.tt(rho[:], lamR[:], dt[:], ALU.mult)
        B.tt(rho[:], ecst[:], rho[:], ALU.pow, eng="pool")
        tau = sb("s5tau", [128, LT])
        self.s.add("pool", lambda e: e.iota(tau[:], [[1, LT]], base=0, channel_multiplier=0,
                                            allow_small_or_imprecise_dtypes=True), writes=[tau[:]])
        self.COS = sb("COS", [128, NG, LT])
        self.SINS = sb("SINS", [128, NG, LT])
        n_el = NG * LT
        y = av([NG, LT])
        yi = av([NG, LT])
        fr = av([NG, LT])
        msk = av([NG, LT])
        assert self._ar + 4 * NG <= ARENA
        B.tt(y, bcast_last(th[:], LT), bcast_mid(tau[:], NG), ALU.mult)
        B.ts(y, y, 1.0 / TWO_PI, ALU.mult)
        for (dst, shift) in ((self.SINS, 0.0), (self.COS, 0.25)):
            if shift:
                B.ts(yi, y, shift, ALU.add)
                src = yi
            else:
                src = y
            B.ts(fr, src, 12582912.0, ALU.add)
            B.ts(fr, fr, 12582912.0, ALU.subtract)
            B.tt(fr, src, fr, ALU.subtract)
            B.ts(msk, fr, 0.5, ALU.is_gt)
            B.tt(fr, fr, msk, ALU.subtract)
            B.ts(msk, fr, -0.5, ALU.is_lt)
            B.tt(fr, fr, msk, ALU.add)
            B.act(dst[:], fr, AF.Sin, scale=6.28318)
        B.ts(self.SINS[L], self.SINS[L], -1.0, ALU.mult)
        self.RHO0 = sb("RHO0", [128, NG, 64])
        B.copy(self.RHO0[:], bcast_last(rho[:], 64))
        B.memset(self.RHO0[:, :, 0:1], 0.0)
        def coef(name, col, with_rho):
            R = sb(name + "R", [128, NG])
            Is = sb(name + "I", [128, NG])
            if with_rho:
                B.tt(R[:], rho[:], self.COS[:, :, col], ALU.mult)
                B.tt(Is[:], rho[:], self.SINS[:, :, col], ALU.mult)
                B.ts(Is[:], Is[:], -1.0, ALU.mult)
            else:
                B.copy(R[:], self.COS[:, :, col])
                B.ts(Is[:], self.SINS[:, :, col], -1.0, ALU.mult)
            return R, Is
        self.cA = coef("cA", 1, True)
        self.cE16 = coef("cE16", 16, True)
        self.cE64 = coef("cE64", 64, True)
        self.cF63 = coef("cF63", 63, False)
        self.cF3 = coef("cF3", 3, False)
        AR, AIs = self.cA
        nr = av([NG])
        ni = av([NG])
        t1 = av([NG])
        t2 = av([NG])
        CR = sb("s5CR", [128, NG])
        CIs = sb("s5CIs", [128, NG])
        B.ts(nr, AR[:], -1.0, ALU.add)
        B.copy(ni, AIs[:])
        B.ts(ni[0:64], ni[0:64], -1.0, ALU.mult)
        B.tt(t1, lamR[:], lamR[:], ALU.mult)
        B.tt(t2, lamI[:], lamI[:], ALU.mult)
        B.tt(t1, t1, t2, ALU.add)
        B.recip(t1, t1)
        B.tt(CR[:], nr, lamR[:], ALU.mult)
        B.tt(t2, ni, lamI[:], ALU.mult)
        B.tt(CR[:], CR[:], t2, ALU.add)
        B.tt(CR[:], CR[:], t1, ALU.mult)
        B.tt(CIs[:], ni, lamR[:], ALU.mult)
        B.tt(t2, nr, lamI[:], ALU.mult)
        B.tt(CIs[:], CIs[:], t2, ALU.subtract)
        B.tt(CIs[:], CIs[:], t1, ALU.mult)
        B.ts(CIs[H], CIs[H], -1.0, ALU.mult)
        self._ar = 0
        XB = av([NG, 128])
        Bx = av([NG, 16])
        By = av([NG, 16])
        bre = self.B_re.rearrange("g p c -> p g c")
        bim = self.B_im.rearrange("g p c -> p g c")
        B.dma(Bx[0:64], bre)
        B.dma(Bx[64:128], bim)
        B.dma(By[0:64], bim)
        B.dma(By[64:128], bre)
        Bb = av([NG, 16])
        Bbs = av([NG, 16])
        tmp = av([NG, 16])
        B.tt(Bb, Bx, bcast_last(CR[:], 16), ALU.mult)
        B.tt(tmp, By, bcast_last(CIs[:], 16), ALU.mult)
        B.tt(Bb, Bb, tmp, ALU.add)
        B.tt(Bbs, By, bcast_last(CR[:], 16), ALU.mult)
        B.tt(tmp, Bx, bcast_last(CIs[:], 16), ALU.mult)
        B.tt(Bbs, Bbs, tmp, ALU.subtract)
        self.Bblk = sb("Bblk", [128, NG, 128], BF16)
        self.Bblks = sb("Bblks", [128, NG, 128], BF16)
        dd = lambda base: mkap(base, [list(base.ap[0]), [8 * 128, 4], [128 + 16, 8], [1, 16]])
        ss_ = lambda base: mkap(base, [list(base.ap[0]), [8 * 16, 4], [16, 8], [1, 16]])
        for (srcb, dstb) in ((Bb, self.Bblk), (Bbs, self.Bblks)):
            B.memset(XB, 0.0)
            B.copy(dd(XB[:, 0:1, 0:1]), ss_(srcb[:, 0:1, 0:1]))
            for g4 in range(NG // 4):
                for q in range(4):
                    g = g4 * 4 + q
                    B.tr(self.ptf[:, q * 128:(q + 1) * 128], XB[:, g, :], self.ident_f[:])
                B.copy(dstb[:, g4 * 4:(g4 + 1) * 4, :], self.ptf[:].rearrange("p (q c) -> p q c", q=4))
        Cn = av([4, 128])
        Cn2 = av([4, 128])
        cre = self.C_re.rearrange("(j q) p -> q j p", q=128)
        cim = self.C_im.rearrange("(j q) p -> q j p", q=128)
        B.dma(Cn[:, :, 0:64], cre)
        B.dma(Cn[:, :, 64:128], cim)
        B.dma(Cn2[:, :, 0:64], cim)
        B.dma(Cn2[:, :, 64:128], cre)
        CT = av([4, 128])
        CT2 = av([4, 128])
        for (srcc, dstc) in ((Cn, CT), (Cn2, CT2)):
            for j in range(4):
                B.tr(self.ptf[:, j * 128:(j + 1) * 128], srcc[:, j, :], self.ident_f[:])
            B.copy(dstc, self.ptf[:].rearrange("p (q c) -> p q c", q=4))
        self.Cblk1 = sb("Cblk1", [128, NG, 128], BF16)
        self.Cblk2 = sb("Cblk2", [128, NG, 128], BF16)
        B.memset(self.Cblk1[:], 0.0)
        B.memset(self.Cblk2[:], 0.0)
        cs_ = lambda base: mkap(base, [list(base.ap[0]), [128, 4], [16, 8], [1, 16]])
        B.copy(dd(self.Cblk1[H, 0:1, 0:1]), cs_(CT[H, 0:1, 0:1]))
        B.ts(dd(self.Cblk1[L, 0:1, 0:1]), cs_(CT[L, 0:1, 0:1]), -1.0, ALU.mult)
        B.ts(dd(self.Cblk2[H, 0:1, 0:1]), cs_(CT2[H, 0:1, 0:1]), -1.0, ALU.mult)
        B.copy(dd(self.Cblk2[L, 0:1, 0:1]), cs_(CT2[L, 0:1, 0:1]))
        self.dbg("COS", self.COS[:])
        self.dbg("SINS", self.SINS[:])
        self.dbg("Bb", Bb)
        self.dbg("CT", CT)

    def av(self, dims, dt=None):
        n = 1
        for d in dims:
            n *= d
        if dt is BF16:
            assert n % 2 == 0 and self._ar % 1 == 0
            steps, st = [], 1
            for d in reversed(dims):
                steps.append([st, d])
                st *= d
            base = mkap(self.arena[:, 0:1], [[ARENA, 128], [1, n // 2]], offset_add=self._ar).bitcast(BF16)
            v = mkap(base, [list(base.ap[0])] + steps[::-1])
            self._ar += n // 2
            return v
        steps, st = [], 1
        for d in reversed(dims):
            steps.append([st, d])
            st *= d
        v = mkap(self.arena[:, 0:1], [[ARENA, 128]] + steps[::-1], offset_add=self._ar)
        self._ar += n
        assert self._ar <= ARENA, self._ar
        return v

    def alloc_act(self):
        sb = self.sb
        self.xtok = sb("xtok", [128, 4, D])
        self.hn = sb("hn", [128, D], BF16)
        self.junk = self.hn
        self.ssq = sb("ssq", [128, 8])
        self.rsq = sb("rsq", [128, 8])
        self.hnT = sb("hnT", [128, 8, 512], BF16)
        self.gs = sb("gs", [128, 4, 512], BF16)
        self.qk = sb("qk", [128, 8, 512], BF16)
        self.qT = self.qk[:, 0:4, :]
        self.kT = self.qk[:, 4:8, :]
        self.vT = sb("vT", [128, 4, 512], BF16)
        self.zs = sb("zs", [128, 4, 512], BF16)
        self.ogT = sb("ogT", [128, 4, 512], BF16)
        self.mixT = self.qk
        self.S = [sb("S0", [128, NH, 128])] * 2
        self.Sbf = [sb("Sbf0", [128, NH, 128], BF16)] * 2
        self.convc = sb("convc", [128, 12, 3])
        self.carry = sb("s5carry", [128, NG])
        self.glast = sb("s5glast", [128, NG])
        self.swt = sb("s5swt", [128, NG])
        self.wab = sb("wab", [128, 8, 8], BF16)
        g = lambda n, shp, dt=F32: sb(n, shp, dt)
        self.g_t = g("g_t", [64, 4]); self.g_g = g("g_g", [64, 4]); self.g_beta = g("g_beta", [64, 4])
        self.g_nbeta = g("g_nbeta", [64, 4]); self.g_gc = g("g_gc", [64, 4]); self.g_egc = g("g_egc", [64, 4])
        self.g_negc = g("g_negc", [64, 4]); self.g_egl = g("g_egl", [128, 4]); self.g_bd = g("g_bd", [64, 4])
        self.g_DTi = g("g_DTi", [64, 4, 64])
        self.g_X = g("g_X", [64, 4, 64], BF16); self.g_XT = g("g_XT", [64, 4, 64], BF16)
        self.g_P = [g("g_P%d" % i, [64, 4, 64], BF16) for i in range(2)]
        self.g_PT = [g("g_PT%d" % i, [64, 4, 64], BF16) for i in range(2)]
        self.g_R = g("g_R", [64, 4, 64]); self.g_Rb = g("g_Rb", [64, 4, 64], BF16); self.g_ZT = g("g_ZT", [64, 4, 64], BF16)
        self.g_QK = g("g_QK", [64, 4, 64], BF16)
        self.g_vtok = g("g_vtok", [64, 4, 128], BF16); self.g_ktok = g("g_ktok", [64, 4, 128], BF16)
        self.g_r = g("g_r", [64, 4, 128], BF16); self.g_vnew = g("g_vnew", [64, 4, 128], BF16)
        self.g_vs = g("g_vs", [64, 4, 128], BF16)

        self.g_oss = g("g_oss", [64, 4]); self.g_on = g("g_on", [64, 4, 128], BF16)
        self.memset(self.convc[:], 0.0)
        self.sb_tmp32a = sb("tmp32a", [128, NG])
        self.sgt = sb("sgt", [128, 512])
        self.memset(self.carry[:], 0.0)
        self.memset(self.S[0][:], 0.0)
        self.memset(self.Sbf[0][:], 0.0)

    def phase_a(self, blocks):
        for b, (c0, n, src) in enumerate(blocks):
            xt = self.xtok[:, b, :]
            self.dma(xt[:n], src)
            self.prenorm_T(xt, b, c0, n, self.gpre)

    def prenorm_T(self, xt, b, c0, n, gain):
        ss = self.ssq[:, b:b + 1]
        rs = self.rsq[:, b:b + 1]
        self.act(self.junk[:n], xt[:n], AF.Square, accum=ss[:n])
        self.act(rs[:n], ss[:n], AF.Sqrt, scale=1.0 / D, bias=self.epsb[:n])
        self.recip(rs[:n], rs[:n])
        self.ts(self.hn[:n], xt[:n], rs[:n], ALU.mult)
        pt3 = self.ptb[:].rearrange("p (k c) -> p k c", k=8)
        for kc in range(8):
            self.tr(pt3[:, kc, 0:n], self.hn[:n, kc * 128:(kc + 1) * 128], self.ident_bf[:n, :n])
        g2 = mkap(gain[:, 0:1, 0:1], [[8, 128], [1, 8], [0, n]])
        self.tt(self.hnT[:, :, c0:c0 + n], pt3[:, :, 0:n], g2, ALU.mult)

    def proj_fm(self, w, ncc, rhs_of_kc, KC, T, consumer, cc0=0):
        for cc in range(ncc):
            p = self.next_pm()
            for kc in range(KC):
                self.mm(p[:, 0:T], w[:, kc, cc * 128:(cc + 1) * 128], rhs_of_kc(kc),
                        start=(kc == 0), stop=(kc == KC - 1))
            consumer(cc0 + cc, p[:, 0:T])

    def swap_halves(self, src):
        self.dma(self.swt[0:64, :], src[64:128, :])
        self.dma(self.swt[64:128, :], src[0:64, :])
        return self.swt[:]

    def s5_phase(self, T, kind, last=False):
        self._ar = 0
        A = self.av([NG, 64])
        Bf = self.av([NG, 64])
        G1 = self.av([NG, 64], BF16)
        G2 = self.av([NG, 64], BF16)
        uf = self.av([4, 512])
        ub = self.av([4, 512], BF16)
        tq = A.rearrange("p g c -> p (g c)").rearrange("p (j c) -> p j c", j=4)
        wu = self.wload(self.w_in[:, 0:512], D, 512)

        import os
        stop = int(os.environ.get("K_S5STOP", "99"))

        def cons(j, p):
            if stop == -2:
                return
            self.copy(uf[:, j, 0:T], p, eng="act")
            if stop == -3:
                return
            self.copy(ub[:, j, 0:T], uf[:, j, 0:T])
        if stop == -1:
            self.dbg("ys", wu)
            return
        self.proj_fm(wu, 4, lambda kc: self.hnT[:, kc, 0:T], 8, T, cons)
        if T < 64:
            self.memset(ub[:, :, T:64], 0.0)
            self.memset(uf[:, :, T:64], 0.0)
        nch = (T + 63) // 64
        if stop <= 0:
            self.dbg("ys", uf)
            return
        for ch in range(nch):
            col0 = ch * 64
            if kind == "sample":
                cosv = lambda g0: mkap(self.COS[:, g0:g0 + 1, 0:1], [[NG * LT, 128], [LT, 8], [0, 16], [1, 4]])
                sinv = lambda g0: mkap(self.SINS[:, g0:g0 + 1, 0:1], [[NG * LT, 128], [LT, 8], [0, 16], [1, 4]])
                v4 = lambda ap: ap.rearrange("p g (s t) -> p g s t", t=4)
                cos_all = mkap(self.COS[:, 0:1, 0:1], [[NG * LT, 128], [LT, NG], [0, 16], [1, 4]])
                sin_all = mkap(self.SINS[:, 0:1, 0:1], [[NG * LT, 128], [LT, NG], [0, 16], [1, 4]])
                rho = self.RHO0
            else:
                cosv = lambda g0: self.COS[:, g0:g0 + 8, 0:64]
                sinv = lambda g0: self.SINS[:, g0:g0 + 8, 0:64]
                v4 = lambda ap: ap
                cos_all = self.COS[:, :, 0:64]
                sin_all = self.SINS[:, :, 0:64]
                rho = self.RHO0
            for gb in range(4):
                pb = self.next_psm()
                pbs = self.next_psm()
                pb3 = pb[:].rearrange("p (g c) -> p g c", g=8)
                pbs3 = pbs[:].rearrange("p (g c) -> p g c", g=8)
                for r in range(8):
                    g = gb * 8 + r
                    self.mm(pb3[:, r, :], self.Bblk[:, g, :], ub[:, gb, col0:col0 + 64])
                for r in range(8):
                    g = gb * 8 + r
                    self.mm(pbs3[:, r, :], self.Bblks[:, g, :], ub[:, gb, col0:col0 + 64])
                self.tt(v4(A[:, gb * 8:(gb + 1) * 8, :]), v4(pb3), cosv(gb * 8), ALU.mult)
                self.tt(v4(Bf[:, gb * 8:(gb + 1) * 8, :]), v4(pbs3), sinv(gb * 8), ALU.mult)
            if stop == 1:
                return
            self.tt(A, A, Bf, ALU.add)
            if kind == "sample":
                a0 = mkap(A[:, 0:1, 0:1], [list(A.ap[0]), [64, NG], [4, 16]])
                self.tt(a0, a0, self.h0c[:], ALU.add)
            else:
                self.tt(A[:, :, 0], A[:, :, 0], self.carry[:], ALU.add)
            self.scan(Bf.rearrange("p g c -> p (g c)"), rho[:].rearrange("p g c -> p (g c)"),
                      A.rearrange("p g c -> p (g c)"))
            if stop == 2:
                return
            self.tt(v4(G1), v4(Bf), cos_all, ALU.mult)
            self.tt(v4(G2), v4(Bf), sin_all, ALU.mult)
            py = self.next_pm()
            py3 = py[:, 0:256].rearrange("p (j c) -> p j c", j=4)
            for j in range(4):
                for r in range(8):
                    g = j * 8 + r
                    self.mm(py3[:, j, :], self.Cblk1[:, g, :], G1[:, g, :], start=(r == 0), stop=False)
                    self.mm(py3[:, j, :], self.Cblk2[:, g, :], G2[:, g, :], start=False, stop=(r == 7))
            if stop == 3:
                return
            ucol = uf[:, :, col0:col0 + 64]
            self.tt(tq[:, :, 0:64], ucol, bcast_last(self.s5Dp[:, :, 0], 64), ALU.mult)
            self.tt(ucol, tq[:, :, 0:64], py3, ALU.add)
            if stop == 4:
                return
            if kind == "sample":
                self.s5_sample_final(Bf)
            else:
                lastcol = 15 if kind == "meta" else 63
                self.copy(self.glast[:], Bf[:, :, lastcol])
                gsw = self.swap_halves(self.glast[:])
                fin = last and ch == nch - 1
                ER, EIs = self.cF63 if fin else (self.cE16 if kind == "meta" else self.cE64)
                t1 = self.sb_tmp32a[:]
                self.tt(t1, gsw, EIs[:], ALU.mult)
                self.tt(self.carry[:], self.glast[:], ER[:], ALU.mult)
                self.tt(self.carry[:], self.carry[:], t1, ALU.add)
                if fin:
                    self.s.add("sp", lambda e: e.dma_start(out=self.p_re.rearrange("g p -> p g"), in_=self.carry[0:64, :],
                                                           allow_slow_non_contiguous=True),
                               reads=[self.carry[0:64, :]], writes=[self.p_re], dma=True)
                    self.s.add("sp", lambda e: e.dma_start(out=self.p_im.rearrange("g p -> p g"), in_=self.carry[64:128, :],
                                                           allow_slow_non_contiguous=True),
                               reads=[self.carry[64:128, :]], writes=[self.p_im], dma=True)
        if stop == 5:
            return
        ys = uf[:, :, 0:T]
        t = tq[:, :, 0:T]
        self.act(t, ys, AF.Square)
        self.ts(t, t, 0.044715, ALU.mult, 1.0, ALU.add)
        self.tt(t, t, ys, ALU.mult)
        self.act(t, t, AF.Sigmoid, scale=1.5957691216057308)
        self.tt(self.gs[:, :, 0:T], t, ys, ALU.mult)
        self.dbg("ys", uf)
        self.dbg("gs", self.gs[:])

    def s5_sample_init(self):
        self._ar = 0
        nat = self.av([4, 128])
        nats = self.av([4, 128])
        H0 = self.av([4, 128])
        H0s = self.av([4, 128])
        t = self.av([NG, 16])
        re = self.s5re_in.rearrange("(c q) p -> q c p", q=128)
        im = self.s5im_in.rearrange("(c q) p -> q c p", q=128)
        self.dma(nat[:, :, 0:64], re)
        self.dma(nat[:, :, 64:128], im)
        self.dma(nats[:, :, 0:64], im)
        self.dma(nats[:, :, 64:128], re)
        for (src, dst) in ((nat, H0), (nats, H0s)):
            for c in range(4):
                self.tr(self.ptf[:, c * 128:(c + 1) * 128], src[:, c, :], self.ident_f[:])
            self.copy(dst, self.ptf[:].rearrange("p (c q) -> p c q", c=4))
        self.memset(mkap(self.RHO0[:, 0:1, 0:1], [[NG * 64, 128], [64, NG], [4, 16]]), 0.0)
        self.h0c = self.sb("h0c", [128, NG, 16])
        AR, AIs = self.cA
        v = lambda x: mkap(x[:, 0:1, 0:1], [list(x.ap[0]), [1, NG], [128, 4], [32, 4]])
        bc = lambda x: mkap(x[:, 0:1], [[NG, 128], [1, NG], [0, 4], [0, 4]])
        o4 = lambda x: x.rearrange("p g (c s) -> p g c s", c=4)
        self.tt(o4(self.h0c[:]), v(H0), bc(AR), ALU.mult)
        self.tt(o4(t), v(H0s), bc(AIs), ALU.mult)
        self.tt(self.h0c[:], self.h0c[:], t, ALU.add)

    def s5_sample_final(self, Bf):
        saved = self._ar
        self._ar = 0
        G3 = self.av([16, NG])
        nat2 = self.av([4, 128])
        H3 = self.av([16, NG])
        outn = self.av([4, 128])
        src = mkap(Bf[:, 0:1, 0:1], [list(Bf.ap[0]), [4, 16], [64, NG]], offset_add=3)
        self.copy(G3, src)
        G3f = G3.rearrange("p s g -> p (s g)")
        for c in range(4):
            self.tr(self.ptf[:, c * 128:(c + 1) * 128], G3f[:, c * 128:(c + 1) * 128], self.ident_f[:])
        p3 = self.ptf[:].rearrange("p (c q) -> p c q", c=4)
        self.copy(nat2[:, :, 0:64], p3[:, :, 64:128])
        self.copy(nat2[:, :, 64:128], p3[:, :, 0:64], eng="act")
        for c in range(4):
            self.tr(self.ptf[:, c * 128:(c + 1) * 128], nat2[:, c, :], self.ident_f[:])
        FR, FIs = self.cF3
        bc = lambda x: mkap(x[:, 0:1], [[NG, 128], [0, 16], [1, NG]])
        self.tt(H3, self.ptf[:].rearrange("p (s g) -> p s g", s=16), bc(FIs), ALU.mult)
        self.tt(G3, G3, bc(FR), ALU.mult)
        self.tt(H3, H3, G3, ALU.add)
        H3f = H3.rearrange("p s g -> p (s g)")
        for c in range(4):
            self.tr(self.ptf[:, c * 128:(c + 1) * 128], H3f[:, c * 128:(c + 1) * 128], self.ident_f[:])
        self.copy(outn, self.ptf[:].rearrange("p (c q) -> p c q", c=4))
        self.dma(self.s_re.rearrange("(c q) p -> q c p", q=128), outn[:, :, 0:64])
        self.dma(self.s_im.rearrange("(c q) p -> q c p", q=128), outn[:, :, 64:128])
        self._ar = saved

    def gdn_phase(self, T, kind, last=False):
        self._ar = 0
        if kind == "sample":
            qp = self.av([12, 16, 7])
            qpre_in = lambda cc, j: qp[:, cc, :, j:j + 4]
            qpre_out = lambda cc: qp[:, cc, :, 3:7]
            v3 = lambda ap: ap.rearrange("p (s t) -> p s t", t=4)
        else:
            qp = self.av([12, 3 + 512])
            qpre_in = lambda cc, j: qp[:, cc, j:j + T]
            qpre_out = lambda cc: qp[:, cc, 3:3 + T]
            v3 = lambda ap: ap
        tmpc = self.av([512])
        qkf = self.av([512])
        rt = self.av([512])
        sqb = self.av([512], BF16)
        self.g_o1 = self.av([4, 128])
        self.g_o = self.av([4, 128])
        self.g_osq = self.g_o1
        self.g_gm = self.av([4, 64])
        self.g_DTs = self.av([4, 64])
        if kind == "sample":
            nat = self.tail
            self.dma(nat[0:48, :], self.sconv_in)
            for cc in range(12):
                self.tr(self.ptf[:, cc * 48:(cc + 1) * 48] if cc < 10 else self.pm[0][:, (cc - 10) * 48:(cc - 9) * 48],
                        nat[0:48, cc * 128:(cc + 1) * 128], self.ident_f[:48, :48])
            self.copy(qp[:, 0:10, :, 0:3], self.ptf[:, 0:480].rearrange("p (c s j) -> p c s j", c=10, s=16))
            self.copy(qp[:, 10:12, :, 0:3], self.pm[0][:, 0:96].rearrange("p (c s j) -> p c s j", c=2, s=16))
        else:
            self.copy(qp[:, :, 0:3], self.convc[:])
        for un in range(3):
            w = self.wload(self.w_in[:, O_QKV + un * 512:O_QKV + (un + 1) * 512], D, 512)
            self.proj_fm(w, 4, lambda kc: self.hnT[:, kc, 0:T], 8, T,
                         lambda cc, p: self.copy(qpre_out(cc), v3(p), eng="act"), cc0=un * 4)
            if kind == "sample" or last:
                r0, nr = (0, T) if kind == "sample" else (T - 3, 3)
                p = self.next_pm()
                for kc in range(8):
                    self.mm(p[:nr, 0:512], self.hnT[:, kc, r0:r0 + nr], w[:, kc, :], start=(kc == 0), stop=(kc == 7))
                self.copy(self.tail[:nr, un * 512:(un + 1) * 512], p[:nr, 0:512])
        if kind == "sample":
            for sq in range(NSEQ_S):
                self.dma(self.s_conv[sq * 3:(sq + 1) * 3, :], self.tail[sq * 4 + 1:sq * 4 + 4, :])
        elif last:
            self.dma(self.p_conv, self.tail[0:3, :])
        if kind != "sample":
            self.copy(self.convc[:], qp[:, :, T:T + 3])
        w = self.wload(self.w_in[:, O_Z:O_Z + 512], D, 512)
        self.proj_fm(w, 4, lambda kc: self.hnT[:, kc, 0:T], 8, T,
                     lambda cc, p: self.act(self.zs[:, cc, 0:T], p, AF.Silu))
        self.dma(self.wab[:], self.w_in[:, O_AB:O_AB + 8].rearrange("(kc p) c -> p kc c", p=128), eng="pool")
        for cc in range(12):
            tc_ = v3(tmpc[:, 0:T])
            self.ts(tc_, qpre_in(cc, 0), self.cw[:, cc, 0:1], ALU.mult)
            for j in range(1, 4):
                self.stt(tc_, qpre_in(cc, j), self.cw[:, cc, j:j + 1], tc_, ALU.mult, ALU.add)
            h = cc % 4
            if cc >= 8:
                self.act(self.vT[:, h, 0:T], tmpc[:, 0:T], AF.Silu)
                continue
            self.act(qkf[:, 0:T], tmpc[:, 0:T], AF.Silu)
            self.act(sqb[:, 0:T], qkf[:, 0:T], AF.Square)
            p = self.next_pm()
            self.mm(p[:, 0:T], self.ones_bf[:], sqb[:, 0:T])
            self.act(rt[:, 0:T], p[:, 0:T], AF.Sqrt, bias=self.epsb[:])
            self.recip(rt[:, 0:T], rt[:, 0:T])
            if cc < 4:
                self.stt(self.qT[:, h, 0:T], qkf[:, 0:T], 128.0 ** -0.5, rt[:, 0:T], ALU.mult, ALU.mult)
            else:
                self.tt(self.kT[:, h, 0:T], qkf[:, 0:T], rt[:, 0:T], ALU.mult)
        self.dbg("qT", self.qT)
        self.dbg("kT", self.kT)
        self.dbg("vT", self.vT[:])
        import os
        self.gstop = int(os.environ.get("K_GSTOP", "99"))
        if self.gstop <= 1:
            return
        self.memset(self.hn[:], 0.0, eng="pool")
        if kind == "sample":
            chunks = [(sq * 4, 4) for sq in range(NSEQ_S)]
        elif kind == "meta":
            chunks = [(0, 16)]
        else:
            chunks = [(c * 64, 64) for c in range(T // 64)]
        for ci, (col0, c) in enumerate(chunks):
            if kind == "sample":
                si = ci % 2
                S, Sbf = self.S[si], self.Sbf[si]
                self.dma(S[:], self.sgdn_in[ci * 512:(ci + 1) * 512, :].rearrange("(h k) v -> k h v", h=NH))
                self.copy(Sbf[:], S[:], eng="act")
            else:
                S, Sbf = self.S[0], self.Sbf[0]
            self.gdn_chunk(col0, c, S, Sbf)
            if kind == "sample":
                self.dma(self.s_gdn[ci * 512:(ci + 1) * 512, :].rearrange("(h k) v -> k h v", h=NH), S[:])
        if last:
            self.dma(self.p_gdn.rearrange("(h k) v -> k h v", h=NH), self.S[0][:])
        self.dbg("ogT", self.ogT[:])

    def gdn_chunk(self, col0, c, S, Sbf):
        cs = slice(col0, col0 + c)
        H4 = range(NH)
        pab = self.next_pm()
        for kc in range(8):
            self.mm(pab[:c, 0:8], self.hnT[:, kc, cs], self.wab[:, kc, :], start=(kc == 0), stop=(kc == 7))
        t, g, beta = self.g_t, self.g_g, self.g_beta
        self.tt(t[:c], pab[:c, 0:4], self.dtb[:c], ALU.add)
        self.act(t[:c], t[:c], AF.Exp)
        self.act(t[:c], t[:c], AF.Ln, bias=self.onesb[:c])
        self.tt(g[:c], t[:c], self.negA[:c], ALU.mult)
        self.act(beta[:c], pab[:c, 4:8], AF.Exp, scale=-1.0)
        self.ts(beta[:c], beta[:c], 1.0, ALU.add)
        self.recip(beta[:c], beta[:c])
        self.ts(self.g_nbeta[:c], beta[:c], -1.0, ALU.mult)
        pg = self.next_pm()
        self.mm(pg[:, 0:4], self.UT[:c, :], g[:c])
        self.mm(pg[:, 8:12], self.ones_f[:c, :], g[:c])
        self.copy(self.g_gc[:c], pg[:c, 0:4])
        self.act(self.g_egc[:c], pg[:c, 0:4], AF.Exp)
        self.ts(self.g_negc[:c], self.g_egc[:c], -1.0, ALU.mult)
        self.act(self.g_egl[:], pg[:, 8:12], AF.Exp)
        self.tt(t[:c], pg[:c, 8:12], self.g_gc[:c], ALU.subtract)
        self.act(t[:c], t[:c], AF.Exp)
        self.tt(self.g_bd[:c], t[:c], beta[:c], ALU.mult)
        if self.gstop <= 2:
            return
        pD = self.next_pm()
        pD3 = pD[:, 0:256].rearrange("p (h c) -> p h c", h=4)
        for h in H4:
            self.ts(self.g_gm[:c, h, :c], self.UT[:c, :c], g[:c, h:h + 1], ALU.mult)
            self.mm(pD3[:, h, :c], self.SL[:c, :], self.g_gm[:c, h, :c])
        DTs, DTi = self.g_DTs, self.g_DTi
        self.act(DTs[:c, :, :c], pD3[:c, :, :c], AF.Exp)
        self.tt(DTs[:c, :, :c], DTs[:c, :, :c], bcast_mid(self.SU[:c, :c], 4), ALU.mult)
        self.tt(DTi[:c, :, :c], DTs[:c, :, :c], bcast_mid(self.ident_f[:c, :c], 4), ALU.add)
        pk = self.next_pm()
        pk3 = pk[:, 0:256].rearrange("p (h c) -> p h c", h=4)
        pq = self.next_pm()
        pq3 = pq[:, 0:256].rearrange("p (h c) -> p h c", h=4)
        for h in H4:
            self.mm(pk3[:c, h, :c], self.kT[:, h, cs], self.kT[:, h, cs])
        for h in H4:
            self.mm(pq3[:c, h, :c], self.kT[:, h, cs], self.qT[:, h, cs])
        X, XT, R = self.g_X, self.g_XT, self.g_R
        for h in H4:
            self.stt(X[:c, h, :c], pk3[:c, h, :c], self.g_nbeta[:c, h:h + 1], DTs[:c, h, :c], ALU.mult, ALU.mult)
        self.tt(self.g_QK[:c, :, :c], pq3[:c, :, :c], DTi[:c, :, :c], ALU.mult)
        if self.gstop <= 3:
            return
        self.tt(R[:c, :, :c], X[:c, :, :c], bcast_mid(self.ident_f[:c, :c], 4), ALU.add)
        Rb = self.g_Rb
        self.copy(Rb[:c, :, :c], R[:c, :, :c], eng="act")
        n_it = {64: 5, 16: 3, 4: 1}[c]
        pt3 = self.ptb[:, 0:256].rearrange("p (h c) -> p h c", h=4)
        for h in H4:
            self.tr(pt3[:c, h, :c], X[:c, h, :c], self.ident_bf[:c, :c])
        self.copy(XT[:c, :, :c], pt3[:c, :, :c], eng="act")
        P, PT = X, XT
        for it in range(n_it):
            P2, P2T = self.g_P[it % 2], self.g_PT[it % 2]
            lastit = it == n_it - 1
            pb_ = self.next_pm()
            pb3 = pb_[:, 0:256].rearrange("p (h c) -> p h c", h=4)
            for h in H4:
                self.mm(pb3[:c, h, :c], P[:c, h, :c], PT[:c, h, :c])
            self.copy(P2T[:c, :, :c], pb3[:c, :, :c], eng="act")
            if not lastit:
                pa_ = self.next_pm()
                pa3 = pa_[:, 0:256].rearrange("p (h c) -> p h c", h=4)
                for h in H4:
                    self.mm(pa3[:c, h, :c], PT[:c, h, :c], P[:c, h, :c])
                self.copy(P2[:c, :, :c], pa3[:c, :, :c], eng="act")
            pc_ = self.next_pm()
            pc3 = pc_[:, 0:256].rearrange("p (h c) -> p h c", h=4)
            for h in H4:
                self.mm(pc3[:c, h, :c], P2T[:c, h, :c], Rb[:c, h, :c])
            self.tt(R[:c, :, :c], R[:c, :, :c], pc3[:c, :, :c], ALU.add)
            if not lastit:
                self.copy(Rb[:c, :, :c], R[:c, :, :c], eng="act")
            P, PT = P2, P2T
        self.copy(self.g_ZT[:c, :, :c], R[:c, :, :c], eng="act")
        if self.gstop <= 4:
            return
        ptb3 = self.ptb[:].rearrange("p (k c) -> p k c", k=8)
        import os
        gsub = int(os.environ.get("K_GSUB", "99"))
        hn3 = self.hn[:].rearrange("p (k c) -> p k c", k=8)
        self.copy(hn3[:, 0:4, 0:c], self.vT[:, :, cs])
        if gsub <= 1:
            return
        self.copy(hn3[:, 4:8, 0:c], self.kT[:, :, cs])
        if gsub <= 2:
            return
        for k8 in range(8):
            self.tr(ptb3[:, k8, :], hn3[:, k8, :], self.ident_bf[:])
            if gsub == 3 and k8 == 0:
                return
        if gsub <= 4:
            return
        ones_b = mkap(self.ones_bf[:, 0:1], [[128, 128], [0, 8], [0, 128]])
        self.tt(hn3, ptb3, ones_b, ALU.mult)
        self.g_vtok = hn3[:, 0:4, :]
        self.g_ktok = hn3[:, 4:8, :]
        if gsub <= 5:
            return
        if self.gstop <= 5:
            return
        pS = self.next_pm()
        pS3 = pS[:].rearrange("p (h c) -> p h c", h=4)
        pQ = self.next_pm()
        pQ3 = pQ[:].rearrange("p (h c) -> p h c", h=4)
        import os
        g2 = int(os.environ.get("K_GSUB2", "99"))
        for h in H4:
            self.mm(pS3[:c, h, :], self.kT[:, h, cs], Sbf[:, h, :])
        if g2 <= 1:
            return
        for h in H4:
            self.mm(pQ3[:c, h, :], self.qT[:, h, cs], Sbf[:, h, :])
        if g2 <= 2:
            return
        for h in H4:
            self.stt(self.g_r[:c, h, :], pS3[:c, h, :], self.g_negc[:c, h:h + 1], self.g_vtok[:c, h, :],
                     ALU.mult, ALU.add)
        if g2 <= 3:
            return
        pZ = self.next_pm()
        pZ3 = pZ[:].rearrange("p (h c) -> p h c", h=4)
        zmode = os.environ.get("K_ZMODE", "orig")
        if zmode.startswith("n") and zmode[1:].isdigit():
            nn = int(zmode[1:])
            for h in H4:
                self.mm(pZ3[:c, h, 0:nn], self.g_ZT[:c, h, :c], self.g_r[:c, h, 0:nn])
        elif zmode == "blk16":
            for h in H4:
                for j in range(8):
                    self.mm(pZ3[:c, h, 16 * j:16 * (j + 1)], self.g_ZT[:c, h, :c], self.g_r[:c, h, 16 * j:16 * (j + 1)])
        elif zmode == "k128":
            if not hasattr(self, "g_ZTk"):
                self.g_ZTk = self.sb("g_ZTk", [128, 4, 64], BF16)
                self.g_rk = self.sb("g_rk", [128, 4, 128], BF16)
                self.memset(self.g_ZTk[:], 0.0)
                self.memset(self.g_rk[:], 0.0)
            self.copy(self.g_ZTk[:c, :, :c], self.g_ZT[:c, :, :c])
            self.copy(self.g_rk[:c], self.g_r[:c])
            for h in H4:
                self.mm(pZ3[:c, h, :], self.g_ZTk[:, h, :c], self.g_rk[:, h, :])
        elif zmode == "pad":
            if not hasattr(self, "g_ZTp"):
                self.g_ZTp = self.sb("g_ZTp", [64, 4, 128], BF16)
                self.memset(self.g_ZTp[:], 0.0)
            self.copy(self.g_ZTp[:c, :, :c], self.g_ZT[:c, :, :c])
            for h in H4:
                self.mm(pZ3[:, h, :], self.g_ZTp[:c, h, :], self.g_r[:c, h, :])
        else:
            for h in H4:
                self.mm(pZ3[:c, h, :], self.g_ZT[:c, h, :c], self.g_r[:c, h, :])
        if g2 <= 4:
            return
        g3 = int(os.environ.get("K_GSUB3", "99"))
        for h in H4:
            self.stt(self.g_vnew[:c, h, :], pZ3[:c, h, :], beta[:c, h:h + 1], self.ones_bf[:c, :], ALU.mult, ALU.mult)
            if g3 <= 1:
                continue
            self.stt(self.g_vs[:c, h, :], pZ3[:c, h, :], self.g_bd[:c, h:h + 1], self.ones_bf[:c, :], ALU.mult, ALU.mult)
        if g2 <= 5:
            return
        pO = self.next_pm()
        pO3 = pO[:].rearrange("p (h c) -> p h c", h=4)
        for h in H4:
            self.mm(pO3[:c, h, :], self.g_QK[:c, h, :c], self.g_vnew[:c, h, :])
        if g2 <= 6:
            return
        for h in H4:
            self.stt(self.g_o1[:c, h, :], pQ3[:c, h, :], self.g_egc[:c, h:h + 1], self.ones_f[:c, :], ALU.mult, ALU.mult)
        self.tt(self.g_o[:c], self.g_o1[:c], pO3[:c], ALU.add)
        if g2 <= 7:
            return
        pKV = self.next_pm()
        pKV3 = pKV[:].rearrange("p (h c) -> p h c", h=4)
        for h in H4:
            self.mm(pKV3[:, h, :], self.g_ktok[:c, h, :], self.g_vs[:c, h, :])
        if g2 <= 8:
            return
        for h in H4:
            self.stt(S[:, h, :], S[:, h, :], self.g_egl[:, h:h + 1], pKV3[:, h, :], ALU.mult, ALU.add)
        self.copy(Sbf[:], S[:], eng="act")
        if self.gstop <= 6:
            return
        o, osq = self.g_o, self.g_osq
        self.tt(osq[:c], o[:c], o[:c], ALU.mult)
        self.s.add("dve", lambda e: e.tensor_reduce(self.g_oss[:c], osq[:c], AX.X, ALU.add),
                   reads=[osq[:c]], writes=[self.g_oss[:c]])
        self.act(self.g_oss[:c], self.g_oss[:c], AF.Ln, scale=1.0 / 128, bias=self.epsb[:c])
        self.act(self.g_oss[:c], self.g_oss[:c], AF.Exp, scale=-0.5)
        self.tt(self.g_on[:c], o[:c], bcast_last(self.g_oss[:c], 128), ALU.mult)
        for h in H4:
            self.tr(ptb3[:, h, 0:c], self.g_on[:c, h, :], self.ident_bf[:c, :c])
        self.stt(self.ogT[:, :, cs], ptb3[:, 0:4, 0:c], self.gdnw[:, 0, :], self.zs[:, :, cs], ALU.mult, ALU.mult)

    def mix_phase(self, T):
        self._ar = 0
        T1 = self.av([8, 512])
        T2 = self.av([8, 512])
        tmp = self.av([512])
        gsr = lambda kc: self.gs[:, kc, 0:T]
        hnr = lambda kc: self.hnT[:, kc, 0:T]
        ogr = lambda kc: self.ogT[:, kc, 0:T]
        w = self.wload(self.w_gb, S5W, D)
        self.proj_fm(w, 8, gsr, 4, T, lambda f, p: self.act(T1[:, f, 0:T], p, AF.Sigmoid))
        w = self.wload(self.w_ga, S5W, D)
        self.proj_fm(w, 8, gsr, 4, T, lambda f, p: self.tt(T1[:, f, 0:T], p, T1[:, f, 0:T], ALU.mult))

        def c3(f, p):
            self.act(tmp[:, 0:T], p, AF.Sigmoid)
            self.tt(T1[:, f, 0:T], T1[:, f, 0:T], tmp[:, 0:T], ALU.mult)
        for un in range(2):
            w = self.wload(self.w_in[:, O_G + un * 512:O_G + (un + 1) * 512], D, 512)
            self.proj_fm(w, 4, hnr, 8, T, c3, cc0=un * 4)
        for un in range(2):
            w = self.wload(self.w_in[:, O_G + D + un * 512:O_G + D + (un + 1) * 512], D, 512)
            self.proj_fm(w, 4, hnr, 8, T, lambda f, p: self.act(T2[:, f, 0:T], p, AF.Sigmoid), cc0=un * 4)

        def c5(f, p):
            self.tt(T2[:, f, 0:T], p, T2[:, f, 0:T], ALU.mult)
            self.tt(self.mixT[:, f, 0:T], T1[:, f, 0:T], T2[:, f, 0:T], ALU.add)
        w = self.wload(self.w_gdn, 512, D)
        self.proj_fm(w, 8, ogr, 4, T, c5)
        self.dbg("mixT", self.mixT[:])

    def postnorm_residual(self, mo, blocks, gain_bc, dst_of_block):
        for b, (c0, n, src) in enumerate(blocks):
            ss = self.ssq[:, 4 + b:5 + b]
            rs = self.rsq[:, 4 + b:5 + b]
            self.act(self.junk[:n], mo[:n, b, :], AF.Square, accum=ss[:n])
            self.act(rs[:n], ss[:n], AF.Sqrt, scale=1.0 / D, bias=self.epsb[:n])
            self.recip(rs[:n], rs[:n])
            self.stt(mo[:n, b, :], mo[:n, b, :], rs[:n], gain_bc[:n], ALU.mult, ALU.mult)
            self.tt(self.xtok[:n, b, :], self.xtok[:n, b, :], mo[:n, b, :], ALU.add)
            if dst_of_block is not None:
                d = dst_of_block(b)
                if d is not None:
                    self.dma(d[0], self.xtok[d[1], b, :])

    def out_phase(self, T, blocks):
        self._ar = 0
        mo = self.av([4, D])
        for hf in range(2):
            w = self.wload(self.w_out[:, hf * 512:(hf + 1) * 512], D, 512)
            for b, (c0, n, src) in enumerate(blocks):
                p = self.next_pm()
                for kc in range(8):
                    self.mm(p[:n, :], self.mixT[:, kc, c0:c0 + n], w[:, kc, :], start=(kc == 0), stop=(kc == 7))
                self.copy(mo[:n, b, hf * 512:(hf + 1) * 512], p[:n, :], eng="act" if b % 2 else "dve")
        self.dma(self.gbc[:], mkap(self.g_mix_post, [[0, 128], [1, D]]))
        self.postnorm_residual(mo, blocks, self.gbc, None)
        self.dbg("x1", self.xtok[:])
        self.dbg("mo", mo)

    def ffn_phase(self, T, blocks, dst_of_block):
        self._ar = 0
        hff = self.av([22, 512], BF16)
        fo = self.av([4, D])
        sg = self.sgt[:]
        for b, (c0, n, src) in enumerate(blocks):
            self.prenorm_T(self.xtok[:, b, :], b, c0, n, self.gfpre)
        fr = lambda kc: self.hnT[:, kc, 0:T]
        for un in range(6):
            nc_ = 4 if un < 5 else 2
            wg = self.wload(self.w_fg[:, un * 512:un * 512 + nc_ * 128], D, nc_ * 128)
            wu = self.wload(self.w_fu[:, un * 512:un * 512 + nc_ * 128], D, nc_ * 128)
            for q in range(nc_):
                cc = un * 4 + q
                pg = self.next_pm()
                pu = self.next_pm()
                for kc in range(8):
                    self.mm(pg[:, 0:T], wg[:, kc, q * 128:(q + 1) * 128], fr(kc), start=(kc == 0), stop=(kc == 7))
                for kc in range(8):
                    self.mm(pu[:, 0:T], wu[:, kc, q * 128:(q + 1) * 128], fr(kc), start=(kc == 0), stop=(kc == 7))
                self.act(sg[:, 0:T], pg[:, 0:T], AF.Silu)
                self.tt(sg[:, 0:T], sg[:, 0:T], pu[:, 0:T], ALU.mult)
                self.copy(hff[:, cc, 0:T], sg[:, 0:T])
        for hf in range(2):
            acc = [self.pm[b] for b in range(len(blocks))]
            kc_all = 0
            for un in range(3):
                nk = 8 if un < 2 else 6
                w = self.wload(self.w_fd[un * 1024:un * 1024 + nk * 128, hf * 512:(hf + 1) * 512], nk * 128, 512)
                for ki in range(nk):
                    kc = un * 8 + ki
                    for b, (c0, n, src) in enumerate(blocks):
                        self.mm(acc[b][:n, :], hff[:, kc, c0:c0 + n], w[:, ki, :], start=(kc == 0), stop=(kc == 21))
            for b, (c0, n, src) in enumerate(blocks):
                self.copy(fo[:n, b, hf * 512:(hf + 1) * 512], acc[b][:n, :], eng="act" if b % 2 else "dve")
        self.dma(self.gbc[:], mkap(self.g_ffn_post, [[0, 128], [1, D]]))
        self.postnorm_residual(fo, blocks, self.gbc, dst_of_block)
        self.dbg("xout", self.xtok[:])

    def run_all(self):
        self.alloc_act()
        sts = []
        sts.append(("meta", 16, [(0, 16, self.xp[0:16, :])], None))
        for i in range(4):
            r0 = NMETA + i * 512
            blocks = [(b * 128, 128, self.xp[r0 + b * 128:r0 + (b + 1) * 128, :]) for b in range(4)]
            dst = (lambda i_: (lambda b: (self.yp[i_ * 512 + b * 128:i_ * 512 + (b + 1) * 128, :], slice(0, 128))))(i)
            sts.append(("prompt", 512, blocks, dst))
        sts.append(("sample", 64, [(0, 64, self.xs[:, :])], lambda b: (self.ys[:, :], slice(0, 64))))
        sel = self.st_list if self.st_list is not None else list(range(len(sts)))
        for si in sel:
            kind, T, blocks, dst = sts[si]
            last = (si == 4)
            self._dbg_on = (si == sel[-1])
            if kind == "sample":
                self.s5_sample_init()
            ph = self.phases
            self.phase_a(blocks)
            self.dbg("hnT", self.hnT[:])
            if "s5" in ph:
                self.s5_phase(T, kind, last=last)
            if "gdn" in ph:
                self.gdn_phase(T, kind, last=last)
            if "gdnstub" in ph:
                self.memset(self.ogT[:, :, 0:T], 0.0)
            if "mix" in ph:
                self.mix_phase(T)
            if "out" in ph:
                self.out_phase(T, blocks)
            if "ffn" in ph:
                self.ffn_phase(T, blocks, dst)


def make_in_map(inp, c):
    f = lambda a: np.ascontiguousarray(np.asarray(a, dtype=np.float32))
    m = {}
    m["xp"] = f(np.concatenate([inp["meta_tokens"], inp["x_prompt"][c]], axis=0))
    m["xs"] = f(inp["x_sample"][NSEQ_S * c:NSEQ_S * (c + 1)].reshape(NSEQ_S * LS, D))
    sl = slice(NSEQ_S * c, NSEQ_S * (c + 1))
    m["s5re"] = f(inp["state_s5_re"][0, sl].reshape(NSEQ_S * NG, NP))
    m["s5im"] = f(inp["state_s5_im"][0, sl].reshape(NSEQ_S * NG, NP))
    m["sgdn"] = f(inp["state_gdn"][0, sl].reshape(NSEQ_S * NH * 128, 128))
    m["sconv"] = f(inp["state_conv"][0, sl].reshape(NSEQ_S * 3, QKVW))
    m["g_mix_pre"] = f(inp["norm_mix_pre"][0].reshape(1, D))
    m["g_mix_post"] = f(inp["norm_mix_post"][0].reshape(1, D))
    m["g_ffn_pre"] = f(inp["norm_ffn_pre"][0].reshape(1, D))
    m["g_ffn_post"] = f(inp["norm_ffn_post"][0].reshape(1, D))
    m["w_in"] = f(inp["w_in"][0])
    m["A_re"] = f(inp["s5_A_re"][0])
    m["A_im"] = f(inp["s5_A_im"][0])
    m["B_re"] = f(inp["s5_B_re"][0])
    m["B_im"] = f(inp["s5_B_im"][0])
    m["C_re"] = f(inp["s5_C_re"][0].reshape(NG * 16, NP))
    m["C_im"] = f(inp["s5_C_im"][0].reshape(NG * 16, NP))
    m["log_dt"] = f(inp["s5_log_dt"][0].reshape(1, NG))
    m["s5D"] = f(inp["s5_D"][0].reshape(1, S5W))
    m["w_ga"] = f(inp["w_s5_glu_a"][0])
    m["w_gb"] = f(inp["w_s5_glu_b"][0])
    m["conv_w"] = f(inp["gdn_conv_w"][0])
    m["A_log"] = f(inp["gdn_A_log"][0].reshape(1, NH))
    m["dt_bias"] = f(inp["gdn_dt_bias"][0].reshape(1, NH))
    m["gdn_norm"] = f(inp["gdn_norm"][0].reshape(1, 128))
    m["w_gdn"] = f(inp["w_gdn_out"][0])
    m["w_out"] = f(inp["w_out"][0])
    m["w_fg"] = f(inp["w_ffn_gate"][0])
    m["w_fu"] = f(inp["w_ffn_up"][0])
    m["w_fd"] = f(inp["w_ffn_down"][0])
    return m


def kernel(**inputs):
    inp = {k: np.asarray(v) for k, v in inputs.items()}
    prog = Prog()
    n = 8
    in_maps = [make_in_map(inp, c) for c in range(n)]
    res = run_bass_kernel_spmd(prog.nc, in_maps, core_ids=list(range(n)))
    r = res.results
    cat = lambda k: np.concatenate([np.asarray(r[c][k]) for c in range(n)], axis=0)
    stk = lambda k, shp: np.stack([np.asarray(r[c][k]).reshape(shp) for c in range(n)], axis=0)
    y_prompt = stk("yp", (SEQ, D))
    y_sample = cat("ys").reshape(128, LS, D)
    p_re = stk("p_re", (NG, NP))[None]
    p_im = stk("p_im", (NG, NP))[None]
    p_gdn = stk("p_gdn", (NH, 128, 128))[None]
    p_conv = stk("p_conv", (3, QKVW))[None]
    s_re = cat("s_re").reshape(1, 128, NG, NP)
    s_im = cat("s_im").reshape(1, 128, NG, NP)
    s_gdn = cat("s_gdn").reshape(1, 128, NH, 128, 128)
    s_conv = cat("s_conv").reshape(1, 128, 3, QKVW)
    out = (y_prompt, y_sample, p_re, p_im, p_gdn, p_conv, s_re, s_im, s_gdn, s_conv)
    return tuple(np.ascontiguousarray(o, dtype=np.float32) for o in out)
```

```python
import contextlib
import math
import numpy as np
import concourse.bass as bass
import concourse.mybir as mybir
from concourse.bass_utils import run_bass_kernel_spmd

F32 = mybir.dt.float32
BF16 = mybir.dt.bfloat16
AF = mybir.ActivationFunctionType
ALU = mybir.AluOpType
AX = mybir.AxisListType

D = 1024
SEQ = 2048
NMETA = 16
NSEQ_S = 16
LS = 4
S5W = 512
NG = 32
NP = 64
QKVW = 1536
NH = 4
DFF = 2816
INW = 4616
O_QKV = 512
O_Z = 2048
O_AB = 2560
O_G = 2568
EPS = 1e-6
ARENA = 9728
LT = 65
TWO_PI = 2.0 * math.pi


def ap_box(ap):
    b = _box(ap)
    z = mybir.dt.size(ap.dtype)
    return (b[0], b[1], b[2], b[3] * z, b[4] * z)


def _box(ap):
    dims = ap.ap
    off = int(ap.offset)
    space = str(ap.space)
    if space in ("SB", "PSUM"):
        pstep, pcount = dims[0]
        if pstep == 0:
            p0, f0, p1 = 0, off, 128
        else:
            p0 = off // pstep
            f0 = off % pstep
            p1 = p0 + pcount
        lo = hi = f0
        for s, c in dims[1:]:
            if s >= 0:
                hi += (c - 1) * s
            else:
                lo += (c - 1) * s
        return (ap.name, p0, p1, lo, hi + 1)
    lo = hi = off
    for s, c in dims:
        if s >= 0:
            hi += (c - 1) * s
        else:
            lo += (c - 1) * s
    return (ap.name, 0, 1, lo, hi + 1)


import os as _os
PSUM_BANK_READS = _os.environ.get("K_PSUMBANK", "1") != "0"


class Op:
    __slots__ = ("eng", "fn", "reads", "writes", "dma", "deps", "signal", "tick", "sem", "idx")


class Sched:
    ENGS = ("pe", "act", "dve", "pool", "sp")

    def __init__(self, nc, es, n_dma_sems=64):
        self.nc = nc
        self.es = es
        self.ops = []
        self.recs = {}
        self.n_dma_sems = n_dma_sems

    def add(self, eng, fn, reads=(), writes=(), dma=False):
        op = Op()
        op.eng, op.fn, op.dma = eng, fn, dma
        def rbox(a):
            b = ap_box(a)
            if PSUM_BANK_READS and str(a.space) == "PSUM":
                return (b[0], 0, 128, 0, 1 << 30)
            return b
        op.reads = [rbox(a) for a in reads]
        op.writes = [ap_box(a) for a in writes]
        op.deps, op.signal, op.tick, op.sem = set(), False, None, None
        op.idx = len(self.ops)
        self.ops.append(op)
        self._track(op)
        return op

    def _need(self, op, other_idx, kind):
        o = self.ops[other_idx]
        if o is op:
            return
        if (not op.dma) and (not o.dma) and o.eng == op.eng:
            if op.eng == "pe":
                return
        op.deps.add(other_idx)

    def _track(self, op):
        recs = self.recs
        for (name, p0, p1, f0, f1) in op.reads:
            lst = recs.setdefault(name, [])
            for r in lst:
                if r[1] and r[2] < p1 and p0 < r[3] and r[4] < f1 and f0 < r[5]:
                    self._need(op, r[0], "RAW")
            if not op.dma:
                for i in range(len(lst) - 1, -1, -1):
                    r = lst[i]
                    if (not r[1]) and r[2] == p0 and r[3] == p1 and r[4] == f0 and r[5] == f1:
                        o = self.ops[r[0]]
                        if (not o.dma) and o.eng == op.eng:
                            del lst[i]
            lst.append([op.idx, False, p0, p1, f0, f1])
        for (name, p0, p1, f0, f1) in op.writes:
            lst = recs.setdefault(name, [])
            keep = []
            for r in lst:
                if r[2] < p1 and p0 < r[3] and r[4] < f1 and f0 < r[5]:
                    self._need(op, r[0], "WAW" if r[1] else "WAR")
                    if r[2] >= p0 and r[3] <= p1 and r[4] >= f0 and r[5] <= f1:
                        continue
                keep.append(r)
            keep.append([op.idx, True, p0, p1, f0, f1])
            recs[name] = keep

    def emit(self):
        nc = self.nc
        engs = {"pe": nc.tensor, "act": nc.scalar, "dve": nc.vector, "pool": nc.gpsimd, "sp": nc.sync}
        esem = {e: self.es.enter_context(nc.semaphore("sem_" + e)) for e in self.ENGS}
        half = self.n_dma_sems // 2
        dsems = [self.es.enter_context(nc.semaphore("dsem%d" % i)) for i in range(self.n_dma_sems)]
        pool_of = {"sp": list(range(0, half)), "pool": list(range(half, self.n_dma_sems))}
        nd_q = {"sp": 0, "pool": 0}
        for op in self.ops:
            best = {}
            keep = set()
            for d in op.deps:
                o = self.ops[d]
                if o.dma:
                    keep.add(d)
                elif best.get(o.eng, -1) < d:
                    best[o.eng] = d
            keep.update(best.values())
            op.deps = keep
            for d in keep:
                self.ops[d].signal = True
        ticks = {e: 0 for e in self.ENGS}
        dcount = [0] * self.n_dma_sems
        waited = {}
        nd = 0
        n_wait = 0
        dma_final = {}
        for op in self.ops:
            eng = engs[op.eng]
            wl = {}
            for d in op.deps:
                o = self.ops[d]
                key = id(o.sem)
                if key not in wl or wl[key][1] < o.tick:
                    wl[key] = (o.sem, o.tick)
            if op.dma:
                ks = pool_of.get(op.eng, pool_of["sp"])
                k = ks[nd_q.get(op.eng, 0) % len(ks)]
                nd_q[op.eng] = nd_q.get(op.eng, 0) + 1
                nd += 1
                if dcount[k] > 0:
                    key = id(dsems[k])
                    v = dcount[k] * 16
                    if key not in wl or wl[key][1] < v:
                        wl[key] = (dsems[k], v)
                dcount[k] += 1
                op.sem = dsems[k]
                op.tick = dcount[k] * 16
                dma_final[k] = op.tick
            elif op.signal:
                ticks[op.eng] += 1
                op.sem = esem[op.eng]
                op.tick = ticks[op.eng]
            for key, (sem, val) in wl.items():
                wk = (op.eng, key)
                if waited.get(wk, 0) >= val:
                    continue
                waited[wk] = val
                eng.wait_ge(sem, val)
                n_wait += 1
            ins = op.fn(eng)
            if op.dma:
                ins.then_inc(op.sem, 16)
            elif op.signal:
                ins.then_inc(op.sem, 1)
        for k, v in dma_final.items():
            if waited.get(("sp", id(dsems[k])), 0) < v:
                nc.sync.wait_ge(dsems[k], v)
        self.stats = dict(n_ops=len(self.ops), n_wait=n_wait, ticks=dict(ticks), n_dma=nd)
        return self.stats


class Builder:
    def __init__(self, debug=None):
        self.debug = debug or []
        self.nc = bass.Bass("TRN2", target_bir_lowering=False)
        self.es = contextlib.ExitStack()
        self.s = Sched(self.nc, self.es)
        self.dbg_outs = {}
        self._n = 0

    def sb(self, name, shape, dt=F32):
        return self.es.enter_context(self.nc.sbuf_tensor(name, list(shape), dt))

    def ps(self, name, shape, dt=F32):
        return self.es.enter_context(self.nc.psum_tensor(name, list(shape), dt))

    def din(self, name, shape):
        return self.nc.dram_tensor(name, list(shape), F32, kind="ExternalInput").ap()

    def dout(self, name, shape):
        return self.nc.dram_tensor(name, list(shape), F32, kind="ExternalOutput").ap()

    def mm(self, out, lhsT, rhs, start=True, stop=True):
        self.s.add("pe", lambda e: e.matmul(out, lhsT, rhs, start=start, stop=stop),
                   reads=[lhsT, rhs], writes=[out])

    def tr(self, out, in_, ident):
        self.s.add("pe", lambda e: e.transpose(out, in_, ident), reads=[in_, ident], writes=[out])

    def act(self, out, in_, func, bias=None, scale=None, accum=None, eng="act"):
        reads = [in_]
        kw = {}
        if bias is not None:
            kw["bias"] = bias
            if not isinstance(bias, (int, float)):
                reads.append(bias)
        if scale is not None:
            kw["scale"] = scale
            if not isinstance(scale, (int, float)):
                reads.append(scale)
        writes = [out]
        if accum is not None:
            kw["accum_out"] = accum
            writes.append(accum)
        self.s.add("act", lambda e: e.activation(out, in_, func, **kw), reads=reads, writes=writes)

    def tt(self, out, a, b, op, eng="dve"):
        self.s.add(eng, lambda e: e.tensor_tensor(out, a, b, op), reads=[a, b], writes=[out])

    def ts(self, out, a, s1, op0, s2=None, op1=None, eng="dve", accum=None):
        reads = [a]
        if not isinstance(s1, (int, float)):
            reads.append(s1)
        if s2 is not None and not isinstance(s2, (int, float)):
            reads.append(s2)
        writes = [out]
        if accum is not None:
            writes.append(accum)
        if op1 is None:
            self.s.add(eng, lambda e: e.tensor_scalar(out, a, s1, None, op0), reads=reads, writes=writes)
        elif accum is None:
            self.s.add(eng, lambda e: e.tensor_scalar(out, a, s1, s2, op0, op1), reads=reads, writes=writes)
        else:
            self.s.add(eng, lambda e: e.tensor_scalar(out, a, s1, s2, op0, op1, accum_out=accum),
                       reads=reads, writes=writes)

    def stt(self, out, in0, scalar, in1, op0, op1):
        reads = [in0, in1]
        if not isinstance(scalar, (int, float)):
            reads.append(scalar)
        self.s.add("dve", lambda e: e.scalar_tensor_tensor(out, in0, scalar, in1, op0, op1),
                   reads=reads, writes=[out])

    def copy(self, out, in_, eng="dve"):
        if eng == "act":
            self.s.add("act", lambda e: e.copy(out, in_), reads=[in_], writes=[out])
        else:
            self.s.add(eng, lambda e: e.tensor_copy(out, in_), reads=[in_], writes=[out])

    def memset(self, ap, val, eng="dve"):
        self.s.add(eng, lambda e: e.memset(ap, val), writes=[ap])

    def recip(self, out, in_):
        self.s.add("dve", lambda e: e.reciprocal(out, in_), reads=[in_], writes=[out])

    def scan(self, out, d0, d1, init=0.0):
        self.s.add("dve", lambda e: e.tensor_tensor_scan(out, d0, d1, init, ALU.mult, ALU.add),
                   reads=[d0, d1], writes=[out])

    def dma(self, out, in_, eng="sp"):
        self.s.add(eng, lambda e: e.dma_start(out=out, in_=in_), reads=[in_], writes=[out], dma=True)

    def dbg(self, name, ap):
        if name not in self.debug or not getattr(self, "_dbg_on", True):
            return
        shp = list(ap.shape)
        d = self.dout("dbg_" + name, shp)
        self.dma(d, ap, eng="pool" if ap.dtype != F32 else "sp")
        self.dbg_outs[name] = "dbg_" + name


def mkap(ap, dims, offset_add=0):
    return bass.AP(ap.tensor, int(ap.offset) + offset_add, [list(d) for d in dims])


def bcast_last(ap, n):
    return mkap(ap, list(ap.ap) + [[0, n]])


def bcast_mid(ap, n):
    d = list(ap.ap)
    return mkap(ap, [d[0], [0, n]] + d[1:])


class Prog(Builder):
    def __init__(self, debug=None, st_list=None, phases=("s5", "gdn", "mix", "out", "ffn")):
        super().__init__(debug)
        self.st_list = st_list
        self.phases = phases
        self.io()
        self.consts()
        self.s5_setup()
        self.run_all()
        self.stats = self.s.emit()

    def io(self):
        d = self.din
        self.xp = d("xp", [NMETA + SEQ, D])
        self.xs = d("xs", [NSEQ_S * LS, D])
        self.s5re_in = d("s5re", [NSEQ_S * NG, NP])
        self.s5im_in = d("s5im", [NSEQ_S * NG, NP])
        self.sgdn_in = d("sgdn", [NSEQ_S * NH * 128, 128])
        self.sconv_in = d("sconv", [NSEQ_S * 3, QKVW])
        self.g_mix_pre = d("g_mix_pre", [1, D])
        self.g_mix_post = d("g_mix_post", [1, D])
        self.g_ffn_pre = d("g_ffn_pre", [1, D])
        self.g_ffn_post = d("g_ffn_post", [1, D])
        self.w_in = d("w_in", [D, INW])
        self.A_re = d("A_re", [NG, NP])
        self.A_im = d("A_im", [NG, NP])
        self.B_re = d("B_re", [NG, NP, 16])
        self.B_im = d("B_im", [NG, NP, 16])
        self.C_re = d("C_re", [NG * 16, NP])
        self.C_im = d("C_im", [NG * 16, NP])
        self.log_dt = d("log_dt", [1, NG])
        self.s5D = d("s5D", [1, S5W])
        self.w_ga = d("w_ga", [S5W, D])
        self.w_gb = d("w_gb", [S5W, D])
        self.conv_w = d("conv_w", [4, QKVW])
        self.A_log = d("A_log", [1, NH])
        self.dt_bias = d("dt_bias", [1, NH])
        self.gdn_norm = d("gdn_norm", [1, 128])
        self.w_gdn = d("w_gdn", [512, D])
        self.w_out = d("w_out", [D, D])
        self.w_fg = d("w_fg", [D, DFF])
        self.w_fu = d("w_fu", [D, DFF])
        self.w_fd = d("w_fd", [DFF, D])
        o = self.dout
        self.yp = o("yp", [SEQ, D])
        self.ys = o("ys", [NSEQ_S * LS, D])
        self.p_re = o("p_re", [NG, NP])
        self.p_im = o("p_im", [NG, NP])
        self.p_gdn = o("p_gdn", [NH * 128, 128])
        self.p_conv = o("p_conv", [3, QKVW])
        self.s_re = o("s_re", [NSEQ_S * NG, NP])
        self.s_im = o("s_im", [NSEQ_S * NG, NP])
        self.s_gdn = o("s_gdn", [NSEQ_S * NH * 128, 128])
        self.s_conv = o("s_conv", [NSEQ_S * 3, QKVW])

    def rows_to_cols(self, dram, R, C, dst):
        nat = self.nat8
        self.dma(nat[:R, :C], dram)
        nch = C // 128
        for c in range(nch):
            self.tr(self.ptf[:, c * 8:c * 8 + R], nat[:R, c * 128:(c + 1) * 128], self.ident_f[:R, :R])
        pv = mkap(self.ptf[:, 0:1], [[512, 128], [8, nch], [1, R]])
        self.copy(dst, pv)

    def consts(self):
        B = self
        sb, ps = self.sb, self.ps
        self.pm = [ps("pm%d" % i, [128, 512]) for i in range(4)]
        self.ptb = ps("ptb", [128, 1024], BF16)
        self.ptf = ps("ptf", [128, 512])
        self.psm = [ps("psm%d" % i, [128, 512]) for i in range(2)]
        self._pm_i = 0
        self._psm_i = 0
        self.ident_f = sb("ident_f", [128, 128])
        self.ident_bf = sb("ident_bf", [128, 128], BF16)
        self.ones_f = sb("ones_f", [128, 128])
        self.ones_bf = sb("ones_bf", [128, 128], BF16)
        self.UT = sb("UT", [64, 128])
        self.SU = sb("SU", [64, 64])
        self.SL = sb("SL", [64, 128])
        self.tail = sb("tail", [64, QKVW])
        self.nat8 = self.tail
        self.epsb = sb("epsb", [128, 1])
        B.memset(self.ones_f[:], 1.0, eng="pool")
        B.memset(self.ones_bf[:], 1.0, eng="pool")
        B.memset(self.epsb[:], EPS, eng="pool")
        self.onesb = self.ones_f[:, 0:1]
        sel = lambda out, in_, pat, op, cm: self.s.add(
            "pool", lambda e: e.affine_select(out, in_, pat, op, 0.0, base=0, channel_multiplier=cm),
            reads=[in_], writes=[out])
        sel(self.ident_f[:], self.ones_f[:], [[-1, 128]], ALU.is_equal, 1)
        B.memset(self.UT[:], 0.0, eng="pool")
        B.memset(self.SL[:], 0.0, eng="pool")
        sel(self.UT[:, 0:64], self.ones_f[0:64, 0:64], [[1, 64]], ALU.is_ge, -1)
        sel(self.SU[:], self.ones_f[0:64, 0:64], [[1, 64]], ALU.is_gt, -1)
        sel(self.SL[:, 0:64], self.ones_f[0:64, 0:64], [[-1, 64]], ALU.is_gt, 1)
        B.copy(self.ident_bf[:], self.ident_f[:])
        self.gpre = sb("gpre", [128, 8, 1])
        self.gfpre = sb("gfpre", [128, 8, 1])
        self.rows_to_cols(self.g_mix_pre, 1, D, self.gpre[:])
        self.rows_to_cols(self.g_ffn_pre, 1, D, self.gfpre[:])
        self.gbc = sb("gbc", [128, D])
        self.s5Dp = sb("s5Dp", [128, 4, 1])
        self.rows_to_cols(self.s5D, 1, S5W, self.s5Dp[:])
        self.cw = sb("cw", [128, 12, 4])
        self.rows_to_cols(self.conv_w, 4, QKVW, self.cw[:])
        self.gdnw = sb("gdnw", [128, 1, 1])
        self.rows_to_cols(self.gdn_norm, 1, 128, self.gdnw[:])
        self.negA = sb("negA", [128, NH])
        self.dtb = sb("dtb", [128, NH])
        B.dma(self.negA[:], mkap(self.A_log, [[0, 128], [1, NH]]))
        B.dma(self.dtb[:], mkap(self.dt_bias, [[0, 128], [1, NH]]))
        B.act(self.negA[:], self.negA[:], AF.Exp)
        B.ts(self.negA[:], self.negA[:], -1.0, ALU.mult)
        self.NWB = 3
        self.wb = [sb("wb%d" % i, [128, 4096], BF16) for i in range(self.NWB)]
        self._wb_i = 0

    def next_pm(self):
        p = self.pm[self._pm_i % 4]
        self._pm_i += 1
        return p

    def next_psm(self):
        p = self.psm[self._psm_i % 2]
        self._psm_i += 1
        return p

    def wload(self, dram2d, K, C):
        kc = K // 128
        buf = self.wb[self._wb_i % self.NWB]
        self._wb_i += 1
        v = mkap(buf[:, 0:1], [[4096, 128], [C, kc], [1, C]])
        src = dram2d.rearrange("(kc p) c -> p kc c", p=128)
        self.dma(v, src, eng="pool")
        return v

    def s5_setup(self):
        B = self
        sb = self.sb
        H = slice(0, 64)
        L = slice(64, 128)
        self.arena = sb("arena", [128, ARENA])
        ar = self.arena
        self._ar = 0

        def av(dims, dt=None):
            n = 1
            for d in dims:
                n *= d
            steps = []
            st = 1
            for d in reversed(dims):
                steps.append([st, d])
                st *= d
            v = mkap(ar[:, 0:1], [[ARENA, 128]] + steps[::-1], offset_add=self._ar)
            self._ar += n
            return v
        nat = sb("s5nat", [32, 128])
        lamR = sb("lamR", [128, NG])
        lamI = sb("lamI", [128, NG])
        for (src, dst) in ((self.A_re, lamR), (self.A_im, lamI)):
            B.dma(nat[:, 0:64], src)
            B.dma(nat[:, 64:128], src)
            B.tr(self.ptf[:, 0:32], nat[:, :], self.ident_f[:32, :32])
            B.copy(dst[:], self.ptf[:, 0:32])
        dt = sb("s5dt", [128, NG])
        B.dma(dt[:], mkap(self.log_dt, [[0, 128], [1, NG]]))
        ecst = sb("s5e", [128, NG])
        B.memset(ecst[:], math.e, eng="pool")
        B.tt(dt[:], ecst[:], dt[:], ALU.pow, eng="pool")
        th = sb("s5th", [128, NG])
        rho = sb("s5rho", [128, NG])
        B.tt(th[:], lamI[:], dt[:], ALU.mult)
        B.tt(rho[:], lamR[:], dt[:], ALU.mult)
        B.tt(rho[:], ecst[:], rho[:], ALU.pow, eng="pool")
        tau = sb("s5tau", [128, LT])
        self.s.add("pool", lambda e: e.iota(tau[:], [[1, LT]], base=0, channel_multiplier=0,
                                            allow_small_or_imprecise_dtypes=True), writes=[tau[:]])
        self.COS = sb("COS", [128, NG, LT])
        self.SINS = sb("SINS", [128, NG, LT])
        n_el = NG * LT
        y = av([NG, LT])
        yi = av([NG, LT])
        fr = av([NG, LT])
        msk = av([NG, LT])
        assert self._ar + 4 * NG <= ARENA
        B.tt(y, bcast_last(th[:], LT), bcast_mid(tau[:], NG), ALU.mult)
        B.ts(y, y, 1.0 / TWO_PI, ALU.mult)
        for (dst, shift) in ((self.SINS, 0.0), (self.COS, 0.25)):
            if shift:
                B.ts(yi, y, shift, ALU.add)
                src = yi
            else:
                src = y
            B.ts(fr, src, 12582912.0, ALU.add)
            B.ts(fr, fr, 12582912.0, ALU.subtract)
            B.tt(fr, src, fr, ALU.subtract)
            B.ts(msk, fr, 0.5, ALU.is_gt)
            B.tt(fr, fr, msk, ALU.subtract)
            B.ts(msk, fr, -0.5, ALU.is_lt)
            B.tt(fr, fr, msk, ALU.add)
            B.act(dst[:], fr, AF.Sin, scale=6.28318)
        B.ts(self.SINS[L], self.SINS[L], -1.0, ALU.mult)
        self.RHO0 = sb("RHO0", [128, NG, 64])
        B.copy(self.RHO0[:], bcast_last(rho[:], 64))
        B.memset(self.RHO0[:, :, 0:1], 0.0)
        def coef(name, col, with_rho):
            R = sb(name + "R", [128, NG])
            Is = sb(name + "I", [128, NG])
            if with_rho:
                B.tt(R[:], rho[:], self.COS[:, :, col], ALU.mult)
                B.tt(Is[:], rho[:], self.SINS[:, :, col], ALU.mult)
                B.ts(Is[:], Is[:], -1.0, ALU.mult)
            else:
                B.copy(R[:], self.COS[:, :, col])
                B.ts(Is[:], self.SINS[:, :, col], -1.0, ALU.mult)
            return R, Is
        self.cA = coef("cA", 1, True)
        self.cE16 = coef("cE16", 16, True)
        self.cE64 = coef("cE64", 64, True)
        self.cF63 = coef("cF63", 63, False)
        self.cF3 = coef("cF3", 3, False)
        AR, AIs = self.cA
        nr = av([NG])
        ni = av([NG])
        t1 = av([NG])
        t2 = av([NG])
        CR = sb("s5CR", [128, NG])
        CIs = sb("s5CIs", [128, NG])
        B.ts(nr, AR[:], -1.0, ALU.add)
        B.copy(ni, AIs[:])
        B.ts(ni[0:64], ni[0:64], -1.0, ALU.mult)
        B.tt(t1, lamR[:], lamR[:], ALU.mult)
        B.tt(t2, lamI[:], lamI[:], ALU.mult)
        B.tt(t1, t1, t2, ALU.add)
        B.recip(t1, t1)
        B.tt(CR[:], nr, lamR[:], ALU.mult)
        B.tt(t2, ni, lamI[:], ALU.mult)
        B.tt(CR[:], CR[:], t2, ALU.add)
        B.tt(CR[:], CR[:], t1, ALU.mult)
        B.tt(CIs[:], ni, lamR[:], ALU.mult)
        B.tt(t2, nr, lamI[:], ALU.mult)
        B.tt(CIs[:], CIs[:], t2, ALU.subtract)
        B.tt(CIs[:], CIs[:], t1, ALU.mult)
        B.ts(CIs[H], CIs[H], -1.0, ALU.mult)
        self._ar = 0
        XB = av([NG, 128])
        Bx = av([NG, 16])
        By = av([NG, 16])
        bre = self.B_re.rearrange("g p c -> p g c")
        bim = self.B_im.rearrange("g p c -> p g c")
        B.dma(Bx[0:64], bre)
        B.dma(Bx[64:128], bim)
        B.dma(By[0:64], bim)
        B.dma(By[64:128], bre)
        Bb = av([NG, 16])
        Bbs = av([NG, 16])
        tmp = av([NG, 16])
        B.tt(Bb, Bx, bcast_last(CR[:], 16), ALU.mult)
        B.tt(tmp, By, bcast_last(CIs[:], 16), ALU.mult)
        B.tt(Bb, Bb, tmp, ALU.add)
        B.tt(Bbs, By, bcast_last(CR[:], 16), ALU.mult)
        B.tt(tmp, Bx, bcast_last(CIs[:], 16), ALU.mult)
        B.tt(Bbs, Bbs, tmp, ALU.subtract)
        self.Bblk = sb("Bblk", [128, NG, 128], BF16)
        self.Bblks = sb("Bblks", [128, NG, 128], BF16)
        dd = lambda base: mkap(base, [list(base.ap[0]), [8 * 128, 4], [128 + 16, 8], [1, 16]])
        ss_ = lambda base: mkap(base, [list(base.ap[0]), [8 * 16, 4], [16, 8], [1, 16]])
        for (srcb, dstb) in ((Bb, self.Bblk), (Bbs, self.Bblks)):
            B.memset(XB, 0.0)
            B.copy(dd(XB[:, 0:1, 0:1]), ss_(srcb[:, 0:1, 0:1]))
            for g4 in range(NG // 4):
                for q in range(4):
                    g = g4 * 4 + q
                    B.tr(self.ptf[:, q * 128:(q + 1) * 128], XB[:, g, :], self.ident_f[:])
                B.copy(dstb[:, g4 * 4:(g4 + 1) * 4, :], self.ptf[:].rearrange("p (q c) -> p q c", q=4))
        Cn = av([4, 128])
        Cn2 = av([4, 128])
        cre = self.C_re.rearrange("(j q) p -> q j p", q=128)
        cim = self.C_im.rearrange("(j q) p -> q j p", q=128)
        B.dma(Cn[:, :, 0:64], cre)
        B.dma(Cn[:, :, 64:128], cim)
        B.dma(Cn2[:, :, 0:64], cim)
        B.dma(Cn2[:, :, 64:128], cre)
        CT = av([4, 128])
        CT2 = av([4, 128])
        for (srcc, dstc) in ((Cn, CT), (Cn2, CT2)):
            for j in range(4):
                B.tr(self.ptf[:, j * 128:(j + 1) * 128], srcc[:, j, :], self.ident_f[:])
            B.copy(dstc, self.ptf[:].rearrange("p (q c) -> p q c", q=4))
        self.Cblk1 = sb("Cblk1", [128, NG, 128], BF16)
        self.Cblk2 = sb("Cblk2", [128, NG, 128], BF16)
        B.memset(self.Cblk1[:], 0.0)
        B.memset(self.Cblk2[:], 0.0)
        cs_ = lambda base: mkap(base, [list(base.ap[0]), [128, 4], [16, 8], [1, 16]])
        B.copy(dd(self.Cblk1[H, 0:1, 0:1]), cs_(CT[H, 0:1, 0:1]))
        B.ts(dd(self.Cblk1[L, 0:1, 0:1]), cs_(CT[L, 0:1, 0:1]), -1.0, ALU.mult)
        B.ts(dd(self.Cblk2[H, 0:1, 0:1]), cs_(CT2[H, 0:1, 0:1]), -1.0, ALU.mult)
        B.copy(dd(self.Cblk2[L, 0:1, 0:1]), cs_(CT2[L, 0:1, 0:1]))
        self.dbg("COS", self.COS[:])
        self.dbg("SINS", self.SINS[:])
        self.dbg("Bb", Bb)
        self.dbg("CT", CT)

    def av(self, dims, dt=None):
        n = 1
        for d in dims:
            n *= d
        if dt is BF16:
            assert n % 2 == 0 and self._ar % 1 == 0
            steps, st = [], 1
            for d in reversed(dims):
                steps.append([st, d])
                st *= d
            base = mkap(self.arena[:, 0:1], [[ARENA, 128], [1, n // 2]], offset_add=self._ar).bitcast(BF16)
            v = mkap(base, [list(base.ap[0])] + steps[::-1])
            self._ar += n // 2
            return v
        steps, st = [], 1
        for d in reversed(dims):
            steps.append([st, d])
            st *= d
        v = mkap(self.arena[:, 0:1], [[ARENA, 128]] + steps[::-1], offset_add=self._ar)
        self._ar += n
        assert self._ar <= ARENA, self._ar
        return v

    def alloc_act(self):
        sb = self.sb
        self.xtok = sb("xtok", [128, 4, D])
        self.hn = sb("hn", [128, D], BF16)
        self.junk = self.hn
        self.ssq = sb("ssq", [128, 8])
        self.rsq = sb("rsq", [128, 8])
        self.hnT = sb("hnT", [128, 8, 512], BF16)
        self.gs = sb("gs", [128, 4, 512], BF16)
        self.qk = sb("qk", [128, 8, 512], BF16)
        self.qT = self.qk[:, 0:4, :]
        self.kT = self.qk[:, 4:8, :]
        self.vT = sb("vT", [128, 4, 512], BF16)
        self.zs = sb("zs", [128, 4, 512], BF16)
        self.ogT = sb("ogT", [128, 4, 512], BF16)
        self.mixT = self.qk
        self.S = [sb("S0", [128, NH, 128])] * 2
        self.Sbf = [sb("Sbf0", [128, NH, 128], BF16)] * 2
        self.convc = sb("convc", [128, 12, 3])
        self.carry = sb("s5carry", [128, NG])
        self.glast = sb("s5glast", [128, NG])
        self.swt = sb("s5swt", [128, NG])
        self.wab = sb("wab", [128, 8, 8], BF16)
        g = lambda n, shp, dt=F32: sb(n, shp, dt)
        self.g_t = g("g_t", [64, 4]); self.g_g = g("g_g", [64, 4]); self.g_beta = g("g_beta", [64, 4])
        self.g_nbeta = g("g_nbeta", [64, 4]); self.g_gc = g("g_gc", [64, 4]); self.g_egc = g("g_egc", [64, 4])
        self.g_negc = g("g_negc", [64, 4]); self.g_egl = g("g_egl", [128, 4]); self.g_bd = g("g_bd", [64, 4])
        self.g_DTi = g("g_DTi", [64, 4, 64])
        self.g_X = g("g_X", [64, 4, 64], BF16); self.g_XT = g("g_XT", [64, 4, 64], BF16)
        self.g_P = [g("g_P%d" % i, [64, 4, 64], BF16) for i in range(2)]
        self.g_PT = [g("g_PT%d" % i, [64, 4, 64], BF16) for i in range(2)]
        self.g_R = g("g_R", [64, 4, 64]); self.g_Rb = g("g_Rb", [64, 4, 64], BF16); self.g_ZT = g("g_ZT", [64, 4, 64], BF16)
        self.g_QK = g("g_QK", [64, 4, 64], BF16)
        self.g_vtok = g("g_vtok", [64, 4, 128], BF16); self.g_ktok = g("g_ktok", [64, 4, 128], BF16)
        self.g_r = g("g_r", [64, 4, 128], BF16); self.g_vnew = g("g_vnew", [64, 4, 128], BF16)
        self.g_vs = g("g_vs", [64, 4, 128], BF16)

        self.g_oss = g("g_oss", [64, 4]); self.g_on = g("g_on", [64, 4, 128], BF16)
        self.memset(self.convc[:], 0.0)
        self.sb_tmp32a = sb("tmp32a", [128, NG])
        self.sgt = sb("sgt", [128, 512])
        self.memset(self.carry[:], 0.0)
        self.memset(self.S[0][:], 0.0)
        self.memset(self.Sbf[0][:], 0.0)

    def phase_a(self, blocks):
        for b, (c0, n, src) in enumerate(blocks):
            xt = self.xtok[:, b, :]
            self.dma(xt[:n], src)
            self.prenorm_T(xt, b, c0, n, self.gpre)

    def prenorm_T(self, xt, b, c0, n, gain):
        ss = self.ssq[:, b:b + 1]
        rs = self.rsq[:, b:b + 1]
        self.act(self.junk[:n], xt[:n], AF.Square, accum=ss[:n])
        self.act(rs[:n], ss[:n], AF.Sqrt, scale=1.0 / D, bias=self.epsb[:n])
        self.recip(rs[:n], rs[:n])
        self.ts(self.hn[:n], xt[:n], rs[:n], ALU.mult)
        pt3 = self.ptb[:].rearrange("p (k c) -> p k c", k=8)
        for kc in range(8):
            self.tr(pt3[:, kc, 0:n], self.hn[:n, kc * 128:(kc + 1) * 128], self.ident_bf[:n, :n])
        g2 = mkap(gain[:, 0:1, 0:1], [[8, 128], [1, 8], [0, n]])
        self.tt(self.hnT[:, :, c0:c0 + n], pt3[:, :, 0:n], g2, ALU.mult)

    def proj_fm(self, w, ncc, rhs_of_kc, KC, T, consumer, cc0=0):
        for cc in range(ncc):
            p = self.next_pm()
            for kc in range(KC):
                self.mm(p[:, 0:T], w[:, kc, cc * 128:(cc + 1) * 128], rhs_of_kc(kc),
                        start=(kc == 0), stop=(kc == KC - 1))
            consumer(cc0 + cc, p[:, 0:T])

    def swap_halves(self, src):
        self.dma(self.swt[0:64, :], src[64:128, :])
        self.dma(self.swt[64:128, :], src[0:64, :])
        return self.swt[:]

    def s5_phase(self, T, kind, last=False):
        self._ar = 0
        A = self.av([NG, 64])
        Bf = self.av([NG, 64])
        G1 = self.av([NG, 64], BF16)
        G2 = self.av([NG, 64], BF16)
        uf = self.av([4, 512])
        ub = self.av([4, 512], BF16)
        tq = A.rearrange("p g c -> p (g c)").rearrange("p (j c) -> p j c", j=4)
        wu = self.wload(self.w_in[:, 0:512], D, 512)

        import os
        stop = int(os.environ.get("K_S5STOP", "99"))

        def cons(j, p):
            if stop == -2:
                return
            self.copy(uf[:, j, 0:T], p, eng="act")
            if stop == -3:
                return
            self.copy(ub[:, j, 0:T], uf[:, j, 0:T])
        if stop == -1:
            self.dbg("ys", wu)
            return
        self.proj_fm(wu, 4, lambda kc: self.hnT[:, kc, 0:T], 8, T, cons)
        if T < 64:
            self.memset(ub[:, :, T:64], 0.0)
            self.memset(uf[:, :, T:64], 0.0)
        nch = (T + 63) // 64
        if stop <= 0:
            self.dbg("ys", uf)
            return
        for ch in range(nch):
            col0 = ch * 64
            if kind == "sample":
                cosv = lambda g0: mkap(self.COS[:, g0:g0 + 1, 0:1], [[NG * LT, 128], [LT, 8], [0, 16], [1, 4]])
                sinv = lambda g0: mkap(self.SINS[:, g0:g0 + 1, 0:1], [[NG * LT, 128], [LT, 8], [0, 16], [1, 4]])
                v4 = lambda ap: ap.rearrange("p g (s t) -> p g s t", t=4)
                cos_all = mkap(self.COS[:, 0:1, 0:1], [[NG * LT, 128], [LT, NG], [0, 16], [1, 4]])
                sin_all = mkap(self.SINS[:, 0:1, 0:1], [[NG * LT, 128], [LT, NG], [0, 16], [1, 4]])
                rho = self.RHO0
            else:
                cosv = lambda g0: self.COS[:, g0:g0 + 8, 0:64]
                sinv = lambda g0: self.SINS[:, g0:g0 + 8, 0:64]
                v4 = lambda ap: ap
                cos_all = self.COS[:, :, 0:64]
                sin_all = self.SINS[:, :, 0:64]
                rho = self.RHO0
            for gb in range(4):
                pb = self.next_psm()
                pbs = self.next_psm()
                pb3 = pb[:].rearrange("p (g c) -> p g c", g=8)
                pbs3 = pbs[:].rearrange("p (g c) -> p g c", g=8)
                for r in range(8):
                    g = gb * 8 + r
                    self.mm(pb3[:, r, :], self.Bblk[:, g, :], ub[:, gb, col0:col0 + 64])
                for r in range(8):
                    g = gb * 8 + r
                    self.mm(pbs3[:, r, :], self.Bblks[:, g, :], ub[:, gb, col0:col0 + 64])
                self.tt(v4(A[:, gb * 8:(gb + 1) * 8, :]), v4(pb3), cosv(gb * 8), ALU.mult)
                self.tt(v4(Bf[:, gb * 8:(gb + 1) * 8, :]), v4(pbs3), sinv(gb * 8), ALU.mult)
            if stop == 1:
                return
            self.tt(A, A, Bf, ALU.add)
            if kind == "sample":
                a0 = mkap(A[:, 0:1, 0:1], [list(A.ap[0]), [64, NG], [4, 16]])
                self.tt(a0, a0, self.h0c[:], ALU.add)
            else:
                self.tt(A[:, :, 0], A[:, :, 0], self.carry[:], ALU.add)
            self.scan(Bf.rearrange("p g c -> p (g c)"), rho[:].rearrange("p g c -> p (g c)"),
                      A.rearrange("p g c -> p (g c)"))
            if stop == 2:
                return
            self.tt(v4(G1), v4(Bf), cos_all, ALU.mult)
            self.tt(v4(G2), v4(Bf), sin_all, ALU.mult)
            py = self.next_pm()
            py3 = py[:, 0:256].rearrange("p (j c) -> p j c", j=4)
            for j in range(4):
                for r in range(8):
                    g = j * 8 + r
                    self.mm(py3[:, j, :], self.Cblk1[:, g, :], G1[:, g, :], start=(r == 0), stop=False)
                    self.mm(py3[:, j, :], self.Cblk2[:, g, :], G2[:, g, :], start=False, stop=(r == 7))
            if stop == 3:
                return
            ucol = uf[:, :, col0:col0 + 64]
            self.tt(tq[:, :, 0:64], ucol, bcast_last(self.s5Dp[:, :, 0], 64), ALU.mult)
            self.tt(ucol, tq[:, :, 0:64], py3, ALU.add)
            if stop == 4:
                return
            if kind == "sample":
                self.s5_sample_final(Bf)
            else:
                lastcol = 15 if kind == "meta" else 63
                self.copy(self.glast[:], Bf[:, :, lastcol])
                gsw = self.swap_halves(self.glast[:])
                fin = last and ch == nch - 1
                ER, EIs = self.cF63 if fin else (self.cE16 if kind == "meta" else self.cE64)
                t1 = self.sb_tmp32a[:]
                self.tt(t1, gsw, EIs[:], ALU.mult)
                self.tt(self.carry[:], self.glast[:], ER[:], ALU.mult)
                self.tt(self.carry[:], self.carry[:], t1, ALU.add)
                if fin:
                    self.s.add("sp", lambda e: e.dma_start(out=self.p_re.rearrange("g p -> p g"), in_=self.carry[0:64, :],
                                                           allow_slow_non_contiguous=True),
                               reads=[self.carry[0:64, :]], writes=[self.p_re], dma=True)
                    self.s.add("sp", lambda e: e.dma_start(out=self.p_im.rearrange("g p -> p g"), in_=self.carry[64:128, :],
                                                           allow_slow_non_contiguous=True),
                               reads=[self.carry[64:128, :]], writes=[self.p_im], dma=True)
        if stop == 5:
            return
        ys = uf[:, :, 0:T]
        t = tq[:, :, 0:T]
        self.act(t, ys, AF.Square)
        self.ts(t, t, 0.044715, ALU.mult, 1.0, ALU.add)
        self.tt(t, t, ys, ALU.mult)
        self.act(t, t, AF.Sigmoid, scale=1.5957691216057308)
        self.tt(self.gs[:, :, 0:T], t, ys, ALU.mult)
        self.dbg("ys", uf)
        self.dbg("gs", self.gs[:])

    def s5_sample_init(self):
        self._ar = 0
        nat = self.av([4, 128])
        nats = self.av([4, 128])
        H0 = self.av([4, 128])
        H0s = self.av([4, 128])
        t = self.av([NG, 16])
        re = self.s5re_in.rearrange("(c q) p -> q c p", q=128)
        im = self.s5im_in.rearrange("(c q) p -> q c p", q=128)
        self.dma(nat[:, :, 0:64], re)
        self.dma(nat[:, :, 64:128], im)
        self.dma(nats[:, :, 0:64], im)
        self.dma(nats[:, :, 64:128], re)
        for (src, dst) in ((nat, H0), (nats, H0s)):
            for c in range(4):
                self.tr(self.ptf[:, c * 128:(c + 1) * 128], src[:, c, :], self.ident_f[:])
            self.copy(dst, self.ptf[:].rearrange("p (c q) -> p c q", c=4))
        self.memset(mkap(self.RHO0[:, 0:1, 0:1], [[NG * 64, 128], [64, NG], [4, 16]]), 0.0)
        self.h0c = self.sb("h0c", [128, NG, 16])
        AR, AIs = self.cA
        v = lambda x: mkap(x[:, 0:1, 0:1], [list(x.ap[0]), [1, NG], [128, 4], [32, 4]])
        bc = lambda x: mkap(x[:, 0:1], [[NG, 128], [1, NG], [0, 4], [0, 4]])
        o4 = lambda x: x.rearrange("p g (c s) -> p g c s", c=4)
        self.tt(o4(self.h0c[:]), v(H0), bc(AR), ALU.mult)
        self.tt(o4(t), v(H0s), bc(AIs), ALU.mult)
        self.tt(self.h0c[:], self.h0c[:], t, ALU.add)

    def s5_sample_final(self, Bf):
        saved = self._ar
        self._ar = 0
        G3 = self.av([16, NG])
        nat2 = self.av([4, 128])
        H3 = self.av([16, NG])
        outn = self.av([4, 128])
        src = mkap(Bf[:, 0:1, 0:1], [list(Bf.ap[0]), [4, 16], [64, NG]], offset_add=3)
        self.copy(G3, src)
        G3f = G3.rearrange("p s g -> p (s g)")
        for c in range(4):
            self.tr(self.ptf[:, c * 128:(c + 1) * 128], G3f[:, c * 128:(c + 1) * 128], self.ident_f[:])
        p3 = self.ptf[:].rearrange("p (c q) -> p c q", c=4)
        self.copy(nat2[:, :, 0:64], p3[:, :, 64:128])
        self.copy(nat2[:, :, 64:128], p3[:, :, 0:64], eng="act")
        for c in range(4):
            self.tr(self.ptf[:, c * 128:(c + 1) * 128], nat2[:, c, :], self.ident_f[:])
        FR, FIs = self.cF3
        bc = lambda x: mkap(x[:, 0:1], [[NG, 128], [0, 16], [1, NG]])
        self.tt(H3, self.ptf[:].rearrange("p (s g) -> p s g", s=16), bc(FIs), ALU.mult)
        self.tt(G3, G3, bc(FR), ALU.mult)
        self.tt(H3, H3, G3, ALU.add)
        H3f = H3.rearrange("p s g -> p (s g)")
        for c in range(4):
            self.tr(self.ptf[:, c * 128:(c + 1) * 128], H3f[:, c * 128:(c + 1) * 128], self.ident_f[:])
        self.copy(outn, self.ptf[:].rearrange("p (c q) -> p c q", c=4))
        self.dma(self.s_re.rearrange("(c q) p -> q c p", q=128), outn[:, :, 0:64])
        self.dma(self.s_im.rearrange("(c q) p -> q c p", q=128), outn[:, :, 64:128])
        self._ar = saved

    def gdn_phase(self, T, kind, last=False):
        self._ar = 0
        if kind == "sample":
            qp = self.av([12, 16, 7])
            qpre_in = lambda cc, j: qp[:, cc, :, j:j + 4]
            qpre_out = lambda cc: qp[:, cc, :, 3:7]
            v3 = lambda ap: ap.rearrange("p (s t) -> p s t", t=4)
        else:
            qp = self.av([12, 3 + 512])
            qpre_in = lambda cc, j: qp[:, cc, j:j + T]
            qpre_out = lambda cc: qp[:, cc, 3:3 + T]
            v3 = lambda ap: ap
        tmpc = self.av([512])
        qkf = self.av([512])
        rt = self.av([512])
        sqb = self.av([512], BF16)
        self.g_o1 = self.av([4, 128])
        self.g_o = self.av([4, 128])
        self.g_osq = self.g_o1
        self.g_gm = self.av([4, 64])
        self.g_DTs = self.av([4, 64])
        if kind == "sample":
            nat = self.tail
            self.dma(nat[0:48, :], self.sconv_in)
            for cc in range(12):
                self.tr(self.ptf[:, cc * 48:(cc + 1) * 48] if cc < 10 else self.pm[0][:, (cc - 10) * 48:(cc - 9) * 48],
                        nat[0:48, cc * 128:(cc + 1) * 128], self.ident_f[:48, :48])
            self.copy(qp[:, 0:10, :, 0:3], self.ptf[:, 0:480].rearrange("p (c s j) -> p c s j", c=10, s=16))
            self.copy(qp[:, 10:12, :, 0:3], self.pm[0][:, 0:96].rearrange("p (c s j) -> p c s j", c=2, s=16))
        else:
            self.copy(qp[:, :, 0:3], self.convc[:])
        for un in range(3):
            w = self.wload(self.w_in[:, O_QKV + un * 512:O_QKV + (un + 1) * 512], D, 512)
            self.proj_fm(w, 4, lambda kc: self.hnT[:, kc, 0:T], 8, T,
                         lambda cc, p: self.copy(qpre_out(cc), v3(p), eng="act"), cc0=un * 4)
            if kind == "sample" or last:
                r0, nr = (0, T) if kind == "sample" else (T - 3, 3)
                p = self.next_pm()
                for kc in range(8):
                    self.mm(p[:nr, 0:512], self.hnT[:, kc, r0:r0 + nr], w[:, kc, :], start=(kc == 0), stop=(kc == 7))
                self.copy(self.tail[:nr, un * 512:(un + 1) * 512], p[:nr, 0:512])
        if kind == "sample":
            for sq in range(NSEQ_S):
                self.dma(self.s_conv[sq * 3:(sq + 1) * 3, :], self.tail[sq * 4 + 1:sq * 4 + 4, :])
        elif last:
            self.dma(self.p_conv, self.tail[0:3, :])
        if kind != "sample":
            self.copy(self.convc[:], qp[:, :, T:T + 3])
        w = self.wload(self.w_in[:, O_Z:O_Z + 512], D, 512)
        self.proj_fm(w, 4, lambda kc: self.hnT[:, kc, 0:T], 8, T,
                     lambda cc, p: self.act(self.zs[:, cc, 0:T], p, AF.Silu))
        self.dma(self.wab[:], self.w_in[:, O_AB:O_AB + 8].rearrange("(kc p) c -> p kc c", p=128), eng="pool")
        for cc in range(12):
            tc_ = v3(tmpc[:, 0:T])
            self.ts(tc_, qpre_in(cc, 0), self.cw[:, cc, 0:1], ALU.mult)
            for j in range(1, 4):
                self.stt(tc_, qpre_in(cc, j), self.cw[:, cc, j:j + 1], tc_, ALU.mult, ALU.add)
            h = cc % 4
            if cc >= 8:
                self.act(self.vT[:, h, 0:T], tmpc[:, 0:T], AF.Silu)
                continue
            self.act(qkf[:, 0:T], tmpc[:, 0:T], AF.Silu)
            self.act(sqb[:, 0:T], qkf[:, 0:T], AF.Square)
            p = self.next_pm()
            self.mm(p[:, 0:T], self.ones_bf[:], sqb[:, 0:T])
            self.act(rt[:, 0:T], p[:, 0:T], AF.Sqrt, bias=self.epsb[:])
            self.recip(rt[:, 0:T], rt[:, 0:T])
            if cc < 4:
                self.stt(self.qT[:, h, 0:T], qkf[:, 0:T], 128.0 ** -0.5, rt[:, 0:T], ALU.mult, ALU.mult)
            else:
                self.tt(self.kT[:, h, 0:T], qkf[:, 0:T], rt[:, 0:T], ALU.mult)
        self.dbg("qT", self.qT)
        self.dbg("kT", self.kT)
        self.dbg("vT", self.vT[:])
        import os
        self.gstop = int(os.environ.get("K_GSTOP", "99"))
        if self.gstop <= 1:
            return
        self.memset(self.hn[:], 0.0, eng="pool")
        if kind == "sample":
            chunks = [(sq * 4, 4) for sq in range(NSEQ_S)]
        elif kind == "meta":
            chunks = [(0, 16)]
        else:
            chunks = [(c * 64, 64) for c in range(T // 64)]
        for ci, (col0, c) in enumerate(chunks):
            if kind == "sample":
                si = ci % 2
                S, Sbf = self.S[si], self.Sbf[si]
                self.dma(S[:], self.sgdn_in[ci * 512:(ci + 1) * 512, :].rearrange("(h k) v -> k h v", h=NH))
                self.copy(Sbf[:], S[:], eng="act")
            else:
                S, Sbf = self.S[0], self.Sbf[0]
            self.gdn_chunk(col0, c, S, Sbf)
            if kind == "sample":
                self.dma(self.s_gdn[ci * 512:(ci + 1) * 512, :].rearrange("(h k) v -> k h v", h=NH), S[:])
        if last:
            self.dma(self.p_gdn.rearrange("(h k) v -> k h v", h=NH), self.S[0][:])
        self.dbg("ogT", self.ogT[:])

    def gdn_chunk(self, col0, c, S, Sbf):
        cs = slice(col0, col0 + c)
        H4 = range(NH)
        pab = self.next_pm()
        for kc in range(8):
            self.mm(pab[:c, 0:8], self.hnT[:, kc, cs], self.wab[:, kc, :], start=(kc == 0), stop=(kc == 7))
        t, g, beta = self.g_t, self.g_g, self.g_beta
        self.tt(t[:c], pab[:c, 0:4], self.dtb[:c], ALU.add)
        self.act(t[:c], t[:c], AF.Exp)
        self.act(t[:c], t[:c], AF.Ln, bias=self.onesb[:c])
        self.tt(g[:c], t[:c], self.negA[:c], ALU.mult)
        self.act(beta[:c], pab[:c, 4:8], AF.Exp, scale=-1.0)
        self.ts(beta[:c], beta[:c], 1.0, ALU.add)
        self.recip(beta[:c], beta[:c])
        self.ts(self.g_nbeta[:c], beta[:c], -1.0, ALU.mult)
        pg = self.next_pm()
        self.mm(pg[:, 0:4], self.UT[:c, :], g[:c])
        self.mm(pg[:, 8:12], self.ones_f[:c, :], g[:c])
        self.copy(self.g_gc[:c], pg[:c, 0:4])
        self.act(self.g_egc[:c], pg[:c, 0:4], AF.Exp)
        self.ts(self.g_negc[:c], self.g_egc[:c], -1.0, ALU.mult)
        self.act(self.g_egl[:], pg[:, 8:12], AF.Exp)
        self.tt(t[:c], pg[:c, 8:12], self.g_gc[:c], ALU.subtract)
        self.act(t[:c], t[:c], AF.Exp)
        self.tt(self.g_bd[:c], t[:c], beta[:c], ALU.mult)
        if self.gstop <= 2:
            return
        pD = self.next_pm()
        pD3 = pD[:, 0:256].rearrange("p (h c) -> p h c", h=4)
        for h in H4:
            self.ts(self.g_gm[:c, h, :c], self.UT[:c, :c], g[:c, h:h + 1], ALU.mult)
            self.mm(pD3[:, h, :c], self.SL[:c, :], self.g_gm[:c, h, :c])
        DTs, DTi = self.g_DTs, self.g_DTi
        self.act(DTs[:c, :, :c], pD3[:c, :, :c], AF.Exp)
        self.tt(DTs[:c, :, :c], DTs[:c, :, :c], bcast_mid(self.SU[:c, :c], 4), ALU.mult)
        self.tt(DTi[:c, :, :c], DTs[:c, :, :c], bcast_mid(self.ident_f[:c, :c], 4), ALU.add)
        pk = self.next_pm()
        pk3 = pk[:, 0:256].rearrange("p (h c) -> p h c", h=4)
        pq = self.next_pm()
        pq3 = pq[:, 0:256].rearrange("p (h c) -> p h c", h=4)
        for h in H4:
            self.mm(pk3[:c, h, :c], self.kT[:, h, cs], self.kT[:, h, cs])
        for h in H4:
            self.mm(pq3[:c, h, :c], self.kT[:, h, cs], self.qT[:, h, cs])
        X, XT, R = self.g_X, self.g_XT, self.g_R
        for h in H4:
            self.stt(X[:c, h, :c], pk3[:c, h, :c], self.g_nbeta[:c, h:h + 1], DTs[:c, h, :c], ALU.mult, ALU.mult)
        self.tt(self.g_QK[:c, :, :c], pq3[:c, :, :c], DTi[:c, :, :c], ALU.mult)
        if self.gstop <= 3:
            return
        self.tt(R[:c, :, :c], X[:c, :, :c], bcast_mid(self.ident_f[:c, :c], 4), ALU.add)
        Rb = self.g_Rb
        self.copy(Rb[:c, :, :c], R[:c, :, :c], eng="act")
        n_it = {64: 5, 16: 3, 4: 1}[c]
        pt3 = self.ptb[:, 0:256].rearrange("p (h c) -> p h c", h=4)
        for h in H4:
            self.tr(pt3[:c, h, :c], X[:c, h, :c], self.ident_bf[:c, :c])
        self.copy(XT[:c, :, :c], pt3[:c, :, :c], eng="act")
        P, PT = X, XT
        for it in range(n_it):
            P2, P2T = self.g_P[it % 2], self.g_PT[it % 2]
            lastit = it == n_it - 1
            pb_ = self.next_pm()
            pb3 = pb_[:, 0:256].rearrange("p (h c) -> p h c", h=4)
            for h in H4:
                self.mm(pb3[:c, h, :c], P[:c, h, :c], PT[:c, h, :c])
            self.copy(P2T[:c, :, :c], pb3[:c, :, :c], eng="act")
            if not lastit:
                pa_ = self.next_pm()
                pa3 = pa_[:, 0:256].rearrange("p (h c) -> p h c", h=4)
                for h in H4:
                    self.mm(pa3[:c, h, :c], PT[:c, h, :c], P[:c, h, :c])
                self.copy(P2[:c, :, :c], pa3[:c, :, :c], eng="act")
            pc_ = self.next_pm()
            pc3 = pc_[:, 0:256].rearrange("p (h c) -> p h c", h=4)
            for h in H4:
                self.mm(pc3[:c, h, :c], P2T[:c, h, :c], Rb[:c, h, :c])
            self.tt(R[:c, :, :c], R[:c, :, :c], pc3[:c, :, :c], ALU.add)
            if not lastit:
                self.copy(Rb[:c, :, :c], R[:c, :, :c], eng="act")
            P, PT = P2, P2T
        self.copy(self.g_ZT[:c, :, :c], R[:c, :, :c], eng="act")
        if self.gstop <= 4:
            return
        ptb3 = self.ptb[:].rearrange("p (k c) -> p k c", k=8)
        import os
        gsub = int(os.environ.get("K_GSUB", "99"))
        hn3 = self.hn[:].rearrange("p (k c) -> p k c", k=8)
        self.copy(hn3[:, 0:4, 0:c], self.vT[:, :, cs])
        if gsub <= 1:
            return
        self.copy(hn3[:, 4:8, 0:c], self.kT[:, :, cs])
        if gsub <= 2:
            return
        for k8 in range(8):
            self.tr(ptb3[:, k8, :], hn3[:, k8, :], self.ident_bf[:])
            if gsub == 3 and k8 == 0:
                return
        if gsub <= 4:
            return
        ones_b = mkap(self.ones_bf[:, 0:1], [[128, 128], [0, 8], [0, 128]])
        self.tt(hn3, ptb3, ones_b, ALU.mult)
        self.g_vtok = hn3[:, 0:4, :]
        self.g_ktok = hn3[:, 4:8, :]
        if gsub <= 5:
            return
        if self.gstop <= 5:
            return
        pS = self.next_pm()
        pS3 = pS[:].rearrange("p (h c) -> p h c", h=4)
        pQ = self.next_pm()
        pQ3 = pQ[:].rearrange("p (h c) -> p h c", h=4)
        import os
        g2 = int(os.environ.get("K_GSUB2", "99"))
        for h in H4:
            self.mm(pS3[:c, h, :], self.kT[:, h, cs], Sbf[:, h, :])
        if g2 <= 1:
            return
        for h in H4:
            self.mm(pQ3[:c, h, :], self.qT[:, h, cs], Sbf[:, h, :])
        if g2 <= 2:
            return
        for h in H4:
            self.stt(self.g_r[:c, h, :], pS3[:c, h, :], self.g_negc[:c, h:h + 1], self.g_vtok[:c, h, :],
                     ALU.mult, ALU.add)
        if g2 <= 3:
            return
        pZ = self.next_pm()
        pZ3 = pZ[:].rearrange("p (h c) -> p h c", h=4)
        zmode = os.environ.get("K_ZMODE", "orig")
        if zmode.startswith("n") and zmode[1:].isdigit():
            nn = int(zmode[1:])
            for h in H4:
                self.mm(pZ3[:c, h, 0:nn], self.g_ZT[:c, h, :c], self.g_r[:c, h, 0:nn])
        elif zmode == "blk16":
            for h in H4:
                for j in range(8):
                    self.mm(pZ3[:c, h, 16 * j:16 * (j + 1)], self.g_ZT[:c, h, :c], self.g_r[:c, h, 16 * j:16 * (j + 1)])
        elif zmode == "k128":
            if not hasattr(self, "g_ZTk"):
                self.g_ZTk = self.sb("g_ZTk", [128, 4, 64], BF16)
                self.g_rk = self.sb("g_rk", [128, 4, 128], BF16)
                self.memset(self.g_ZTk[:], 0.0)
                self.memset(self.g_rk[:], 0.0)
            self.copy(self.g_ZTk[:c, :, :c], self.g_ZT[:c, :, :c])
            self.copy(self.g_rk[:c], self.g_r[:c])
            for h in H4:
                self.mm(pZ3[:c, h, :], self.g_ZTk[:, h, :c], self.g_rk[:, h, :])
        elif zmode == "pad":
            if not hasattr(self, "g_ZTp"):
                self.g_ZTp = self.sb("g_ZTp", [64, 4, 128], BF16)
                self.memset(self.g_ZTp[:], 0.0)
            self.copy(self.g_ZTp[:c, :, :c], self.g_ZT[:c, :, :c])
            for h in H4:
                self.mm(pZ3[:, h, :], self.g_ZTp[:c, h, :], self.g_r[:c, h, :])
        else:
            for h in H4:
                self.mm(pZ3[:c, h, :], self.g_ZT[:c, h, :c], self.g_r[:c, h, :])
        if g2 <= 4:
            return
        self.tt(self.g_vnew[:c], pZ3[:c], bcast_last(beta[:c], 128), ALU.mult)
        self.tt(self.g_vs[:c], pZ3[:c], bcast_last(self.g_bd[:c], 128), ALU.mult)
        if g2 <= 5:
            return
        pO = self.next_pm()
        pO3 = pO[:].rearrange("p (h c) -> p h c", h=4)
        for h in H4:
            self.mm(pO3[:c, h, :], self.g_QK[:c, h, :c], self.g_vnew[:c, h, :])
        if g2 <= 6:
            return
        self.tt(self.g_o1[:c], pQ3[:c], bcast_last(self.g_egc[:c], 128), ALU.mult)
        self.tt(self.g_o[:c], self.g_o1[:c], pO3[:c], ALU.add)
        if g2 <= 7:
            return
        pKV = self.next_pm()
        pKV3 = pKV[:].rearrange("p (h c) -> p h c", h=4)
        for h in H4:
            self.mm(pKV3[:, h, :], self.g_ktok[:c, h, :], self.g_vs[:c, h, :])
        if g2 <= 8:
            return
        for h in H4:
            self.stt(S[:, h, :], S[:, h, :], self.g_egl[:, h:h + 1], pKV3[:, h, :], ALU.mult, ALU.add)
        self.copy(Sbf[:], S[:], eng="act")
        if self.gstop <= 6:
            return
        o, osq = self.g_o, self.g_osq
        self.tt(osq[:c], o[:c], o[:c], ALU.mult)
        self.s.add("dve", lambda e: e.tensor_reduce(self.g_oss[:c], osq[:c], AX.X, ALU.add),
                   reads=[osq[:c]], writes=[self.g_oss[:c]])
        self.act(self.g_oss[:c], self.g_oss[:c], AF.Ln, scale=1.0 / 128, bias=self.epsb[:c])
        self.act(self.g_oss[:c], self.g_oss[:c], AF.Exp, scale=-0.5)
        self.tt(self.g_on[:c], o[:c], bcast_last(self.g_oss[:c], 128), ALU.mult)
        for h in H4:
            self.tr(ptb3[:, h, 0:c], self.g_on[:c, h, :], self.ident_bf[:c, :c])
        self.stt(self.ogT[:, :, cs], ptb3[:, 0:4, 0:c], self.gdnw[:, 0, :], self.zs[:, :, cs], ALU.mult, ALU.mult)

    def mix_phase(self, T):
        self._ar = 0
        T1 = self.av([8, 512])
        T2 = self.av([8, 512])
        tmp = self.av([512])
        gsr = lambda kc: self.gs[:, kc, 0:T]
        hnr = lambda kc: self.hnT[:, kc, 0:T]
        ogr = lambda kc: self.ogT[:, kc, 0:T]
        w = self.wload(self.w_gb, S5W, D)
        self.proj_fm(w, 8, gsr, 4, T, lambda f, p: self.act(T1[:, f, 0:T], p, AF.Sigmoid))
        w = self.wload(self.w_ga, S5W, D)
        self.proj_fm(w, 8, gsr, 4, T, lambda f, p: self.tt(T1[:, f, 0:T], p, T1[:, f, 0:T], ALU.mult))

        def c3(f, p):
            self.act(tmp[:, 0:T], p, AF.Sigmoid)
            self.tt(T1[:, f, 0:T], T1[:, f, 0:T], tmp[:, 0:T], ALU.mult)
        for un in range(2):
            w = self.wload(self.w_in[:, O_G + un * 512:O_G + (un + 1) * 512], D, 512)
            self.proj_fm(w, 4, hnr, 8, T, c3, cc0=un * 4)
        for un in range(2):
            w = self.wload(self.w_in[:, O_G + D + un * 512:O_G + D + (un + 1) * 512], D, 512)
            self.proj_fm(w, 4, hnr, 8, T, lambda f, p: self.act(T2[:, f, 0:T], p, AF.Sigmoid), cc0=un * 4)

        def c5(f, p):
            self.tt(T2[:, f, 0:T], p, T2[:, f, 0:T], ALU.mult)
            self.tt(self.mixT[:, f, 0:T], T1[:, f, 0:T], T2[:, f, 0:T], ALU.add)
        w = self.wload(self.w_gdn, 512, D)
        self.proj_fm(w, 8, ogr, 4, T, c5)
        self.dbg("mixT", self.mixT[:])

    def postnorm_residual(self, mo, blocks, gain_bc, dst_of_block):
        for b, (c0, n, src) in enumerate(blocks):
            ss = self.ssq[:, 4 + b:5 + b]
            rs = self.rsq[:, 4 + b:5 + b]
            self.act(self.junk[:n], mo[:n, b, :], AF.Square, accum=ss[:n])
            self.act(rs[:n], ss[:n], AF.Sqrt, scale=1.0 / D, bias=self.epsb[:n])
            self.recip(rs[:n], rs[:n])
            self.stt(mo[:n, b, :], mo[:n, b, :], rs[:n], gain_bc[:n], ALU.mult, ALU.mult)
            self.tt(self.xtok[:n, b, :], self.xtok[:n, b, :], mo[:n, b, :], ALU.add)
            if dst_of_block is not None:
                d = dst_of_block(b)
                if d is not None:
                    self.dma(d[0], self.xtok[d[1], b, :])

    def out_phase(self, T, blocks):
        self._ar = 0
        mo = self.av([4, D])
        for hf in range(2):
            w = self.wload(self.w_out[:, hf * 512:(hf + 1) * 512], D, 512)
            for b, (c0, n, src) in enumerate(blocks):
                p = self.next_pm()
                for kc in range(8):
                    self.mm(p[:n, :], self.mixT[:, kc, c0:c0 + n], w[:, kc, :], start=(kc == 0), stop=(kc == 7))
                self.copy(mo[:n, b, hf * 512:(hf + 1) * 512], p[:n, :], eng="act" if b % 2 else "dve")
        self.dma(self.gbc[:], mkap(self.g_mix_post, [[0, 128], [1, D]]))
        self.postnorm_residual(mo, blocks, self.gbc, None)
        self.dbg("x1", self.xtok[:])
        self.dbg("mo", mo)

    def ffn_phase(self, T, blocks, dst_of_block):
        self._ar = 0
        hff = self.av([22, 512], BF16)
        fo = self.av([4, D])
        sg = self.sgt[:]
        for b, (c0, n, src) in enumerate(blocks):
            self.prenorm_T(self.xtok[:, b, :], b, c0, n, self.gfpre)
        fr = lambda kc: self.hnT[:, kc, 0:T]
        for un in range(6):
            nc_ = 4 if un < 5 else 2
            wg = self.wload(self.w_fg[:, un * 512:un * 512 + nc_ * 128], D, nc_ * 128)
            wu = self.wload(self.w_fu[:, un * 512:un * 512 + nc_ * 128], D, nc_ * 128)
            for q in range(nc_):
                cc = un * 4 + q
                pg = self.next_pm()
                pu = self.next_pm()
                for kc in range(8):
                    self.mm(pg[:, 0:T], wg[:, kc, q * 128:(q + 1) * 128], fr(kc), start=(kc == 0), stop=(kc == 7))
                for kc in range(8):
                    self.mm(pu[:, 0:T], wu[:, kc, q * 128:(q + 1) * 128], fr(kc), start=(kc == 0), stop=(kc == 7))
                self.act(sg[:, 0:T], pg[:, 0:T], AF.Silu)
                self.tt(sg[:, 0:T], sg[:, 0:T], pu[:, 0:T], ALU.mult)
                self.copy(hff[:, cc, 0:T], sg[:, 0:T])
        for hf in range(2):
            acc = [self.pm[b] for b in range(len(blocks))]
            kc_all = 0
            for un in range(3):
                nk = 8 if un < 2 else 6
                w = self.wload(self.w_fd[un * 1024:un * 1024 + nk * 128, hf * 512:(hf + 1) * 512], nk * 128, 512)
                for ki in range(nk):
                    kc = un * 8 + ki
                    for b, (c0, n, src) in enumerate(blocks):
                        self.mm(acc[b][:n, :], hff[:, kc, c0:c0 + n], w[:, ki, :], start=(kc == 0), stop=(kc == 21))
            for b, (c0, n, src) in enumerate(blocks):
                self.copy(fo[:n, b, hf * 512:(hf + 1) * 512], acc[b][:n, :], eng="act" if b % 2 else "dve")
        self.dma(self.gbc[:], mkap(self.g_ffn_post, [[0, 128], [1, D]]))
        self.postnorm_residual(fo, blocks, self.gbc, dst_of_block)
        self.dbg("xout", self.xtok[:])

    def run_all(self):
        self.alloc_act()
        sts = []
        sts.append(("meta", 16, [(0, 16, self.xp[0:16, :])], None))
        for i in range(4):
            r0 = NMETA + i * 512
            blocks = [(b * 128, 128, self.xp[r0 + b * 128:r0 + (b + 1) * 128, :]) for b in range(4)]
            dst = (lambda i_: (lambda b: (self.yp[i_ * 512 + b * 128:i_ * 512 + (b + 1) * 128, :], slice(0, 128))))(i)
            sts.append(("prompt", 512, blocks, dst))
        sts.append(("sample", 64, [(0, 64, self.xs[:, :])], lambda b: (self.ys[:, :], slice(0, 64))))
        sel = self.st_list if self.st_list is not None else list(range(len(sts)))
        for si in sel:
            kind, T, blocks, dst = sts[si]
            last = (si == 4)
            self._dbg_on = (si == sel[-1])
            if kind == "sample":
                self.s5_sample_init()
            ph = self.phases
            self.phase_a(blocks)
            self.dbg("hnT", self.hnT[:])
            if "s5" in ph:
                self.s5_phase(T, kind, last=last)
            if "gdn" in ph:
                self.gdn_phase(T, kind, last=last)
            if "gdnstub" in ph:
                self.memset(self.ogT[:, :, 0:T], 0.0)
            if "mix" in ph:
                self.mix_phase(T)
            if "out" in ph:
                self.out_phase(T, blocks)
            if "ffn" in ph:
                self.ffn_phase(T, blocks, dst)


def make_in_map(inp, c):
    f = lambda a: np.ascontiguousarray(np.asarray(a, dtype=np.float32))
    m = {}
    m["xp"] = f(np.concatenate([inp["meta_tokens"], inp["x_prompt"][c]], axis=0))
    m["xs"] = f(inp["x_sample"][NSEQ_S * c:NSEQ_S * (c + 1)].reshape(NSEQ_S * LS, D))
    sl = slice(NSEQ_S * c, NSEQ_S * (c + 1))
    m["s5re"] = f(inp["state_s5_re"][0, sl].reshape(NSEQ_S * NG, NP))
    m["s5im"] = f(inp["state_s5_im"][0, sl].reshape(NSEQ_S * NG, NP))
    m["sgdn"] = f(inp["state_gdn"][0, sl].reshape(NSEQ_S * NH * 128, 128))
    m["sconv"] = f(inp["state_conv"][0, sl].reshape(NSEQ_S * 3, QKVW))
    m["g_mix_pre"] = f(inp["norm_mix_pre"][0].reshape(1, D))
    m["g_mix_post"] = f(inp["norm_mix_post"][0].reshape(1, D))
    m["g_ffn_pre"] = f(inp["norm_ffn_pre"][0].reshape(1, D))
    m["g_ffn_post"] = f(inp["norm_ffn_post"][0].reshape(1, D))
    m["w_in"] = f(inp["w_in"][0])
    m["A_re"] = f(inp["s5_A_re"][0])
    m["A_im"] = f(inp["s5_A_im"][0])
    m["B_re"] = f(inp["s5_B_re"][0])
    m["B_im"] = f(inp["s5_B_im"][0])
    m["C_re"] = f(inp["s5_C_re"][0].reshape(NG * 16, NP))
    m["C_im"] = f(inp["s5_C_im"][0].reshape(NG * 16, NP))
    m["log_dt"] = f(inp["s5_log_dt"][0].reshape(1, NG))
    m["s5D"] = f(inp["s5_D"][0].reshape(1, S5W))
    m["w_ga"] = f(inp["w_s5_glu_a"][0])
    m["w_gb"] = f(inp["w_s5_glu_b"][0])
    m["conv_w"] = f(inp["gdn_conv_w"][0])
    m["A_log"] = f(inp["gdn_A_log"][0].reshape(1, NH))
    m["dt_bias"] = f(inp["gdn_dt_bias"][0].reshape(1, NH))
    m["gdn_norm"] = f(inp["gdn_norm"][0].reshape(1, 128))
    m["w_gdn"] = f(inp["w_gdn_out"][0])
    m["w_out"] = f(inp["w_out"][0])
    m["w_fg"] = f(inp["w_ffn_gate"][0])
    m["w_fu"] = f(inp["w_ffn_up"][0])
    m["w_fd"] = f(inp["w_ffn_down"][0])
    return m


def kernel(**inputs):
    inp = {k: np.asarray(v) for k, v in inputs.items()}
    prog = Prog()
    n = 8
    in_maps = [make_in_map(inp, c) for c in range(n)]
    res = run_bass_kernel_spmd(prog.nc, in_maps, core_ids=list(range(n)))
    r = res.results
    cat = lambda k: np.concatenate([np.asarray(r[c][k]) for c in range(n)], axis=0)
    stk = lambda k, shp: np.stack([np.asarray(r[c][k]).reshape(shp) for c in range(n)], axis=0)
    y_prompt = stk("yp", (SEQ, D))
    y_sample = cat("ys").reshape(128, LS, D)
    p_re = stk("p_re", (NG, NP))[None]
    p_im = stk("p_im", (NG, NP))[None]
    p_gdn = stk("p_gdn", (NH, 128, 128))[None]
    p_conv = stk("p_conv", (3, QKVW))[None]
    s_re = cat("s_re").reshape(1, 128, NG, NP)
    s_im = cat("s_im").reshape(1, 128, NG, NP)
    s_gdn = cat("s_gdn").reshape(1, 128, NH, 128, 128)
    s_conv = cat("s_conv").reshape(1, 128, 3, QKVW)
    out = (y_prompt, y_sample, p_re, p_im, p_gdn, p_conv, s_re, s_im, s_gdn, s_conv)
    return tuple(np.ascontiguousarray(o, dtype=np.float32) for o in out)
```
